# Optimizing a Trainium2 kernel written in Bass

```python
import jax, jax.numpy as jnp
from jax import lax
import numpy as np

D_MODEL = 4096
BATCH = 4
SEQ = 4096
DEPTH = 1
DEC_BATCH = 2
DEC_SEQ = 4096
PAST_LEN = 128

GRID_W = 64
HEAD_DIM = 128
N_Q_HEADS = 16
N_KV_HEADS = 4
ATTN_WIDTH = N_Q_HEADS * HEAD_DIM
KV_WIDTH = N_KV_HEADS * HEAD_DIM
CONV_WIDTH = D_MODEL - ATTN_WIDTH
CONV_GROUPS = 16
CONV_GROUP_DIM = CONV_WIDTH // CONV_GROUPS
CONV_K = 3
D_FF = 4 * D_MODEL
Q_BLOCK = 128
ROPE_THETA = 10000.0
ROPE_AXIS_DIM = HEAD_DIM // 2
EPS = 1e-6
IN_WIDTH = ATTN_WIDTH + 2 * KV_WIDTH + 3 * CONV_WIDTH

kernel_name = "hybrid_parallel_conv_axial_gqa_encoder"


def rmsnorm(x, g):
    xf = x.astype(jnp.float32)
    xf = xf * lax.rsqrt(jnp.mean(xf * xf, axis=-1, keepdims=True) + EPS)
    return (xf * g.astype(jnp.float32)).astype(x.dtype)


def axial_rope_tables(seq_len):
    n_rows = seq_len // GRID_W
    row = jnp.repeat(jnp.arange(n_rows, dtype=jnp.float32), GRID_W)
    col = jnp.tile(jnp.arange(GRID_W, dtype=jnp.float32), n_rows)
    freqs = ROPE_THETA ** (-jnp.arange(0, ROPE_AXIS_DIM, 2, dtype=jnp.float32) / ROPE_AXIS_DIM)
    ang = jnp.stack([row[:, None] * freqs, col[:, None] * freqs], axis=1)
    return jnp.cos(ang), jnp.sin(ang)


def apply_axial_rope(x, cos, sin):
    b, s, h, d = x.shape
    xr = x.astype(jnp.float32).reshape(b, s, h, 2, 2, ROPE_AXIS_DIM // 2)
    x1 = xr[..., 0, :]
    x2 = xr[..., 1, :]
    c = cos[None, :, None]
    sn = sin[None, :, None]
    out = jnp.stack([x1 * c - x2 * sn, x2 * c + x1 * sn], axis=-2)
    return out.reshape(b, s, h, d).astype(x.dtype)


def blocked_gqa(q, k, v):
    b, s, hq, d = q.shape
    g = hq // N_KV_HEADS
    n_blk = s // Q_BLOCK
    scale = HEAD_DIM ** -0.5
    qb = q.reshape(b, n_blk, Q_BLOCK, N_KV_HEADS, g, d).transpose(1, 0, 2, 3, 4, 5)

    def one_block(qi):
        sc = jnp.einsum('bqkgd,bskd->bkgqs', qi, k, preferred_element_type=jnp.float32) * scale
        p = jax.nn.softmax(sc, axis=-1).astype(v.dtype)
        return jnp.einsum('bkgqs,bskd->bqkgd', p, v)

    out = lax.map(one_block, qb)
    return out.transpose(1, 0, 2, 3, 4, 5).reshape(b, s, hq * d)


def centred_depthwise_conv3(u, w):
    up = jnp.pad(u, ((0, 0), (1, 1), (0, 0)))
    return w[0] * up[:, :-2] + w[1] * up[:, 1:-1] + w[2] * up[:, 2:]


def hybrid_layer(x, cos, sin, w_in, q_norm, k_norm, conv_w, attn_grp_norm,
                 conv_grp_norm, w_out, mix_norm, mlp_norm, w_up, w_down):
    b, s, _ = x.shape
    h = rmsnorm(x, mix_norm)
    z = h @ w_in
    cuts = np.cumsum([ATTN_WIDTH, KV_WIDTH, KV_WIDTH, CONV_WIDTH, CONV_WIDTH]).tolist()
    q, k, v, bg, cg, u = jnp.split(z, cuts, axis=-1)

    q = rmsnorm(q.reshape(b, s, N_Q_HEADS, HEAD_DIM), q_norm)
    k = rmsnorm(k.reshape(b, s, N_KV_HEADS, HEAD_DIM), k_norm)
    v = v.reshape(b, s, N_KV_HEADS, HEAD_DIM)
    q = apply_axial_rope(q, cos, sin)
    k = apply_axial_rope(k, cos, sin)
    a = blocked_gqa(q, k, v)
    a = rmsnorm(a.reshape(b, s, N_Q_HEADS, HEAD_DIM),
                attn_grp_norm.reshape(N_Q_HEADS, HEAD_DIM)).reshape(b, s, ATTN_WIDTH)

    c = bg * centred_depthwise_conv3(cg * u, conv_w)
    c = rmsnorm(c.reshape(b, s, CONV_GROUPS, CONV_GROUP_DIM),
                conv_grp_norm.reshape(CONV_GROUPS, CONV_GROUP_DIM)).reshape(b, s, CONV_WIDTH)

    x = x + jnp.concatenate([a, c], axis=-1) @ w_out

    h2 = rmsnorm(x, mlp_norm)
    x = x + jnp.square(jax.nn.relu(h2 @ w_up)) @ w_down
    return x


def run_trunk(x, w_in, q_norm, k_norm, conv_w, attn_grp_norm, conv_grp_norm,
              w_out, mix_norm, mlp_norm, w_up, w_down, final_norm):
    cos, sin = axial_rope_tables(x.shape[1])
    for l in range(DEPTH):
        x = hybrid_layer(x, cos, sin, w_in[l], q_norm[l], k_norm[l], conv_w[l],
                         attn_grp_norm[l], conv_grp_norm[l], w_out[l], mix_norm[l],
                         mlp_norm[l], w_up[l], w_down[l])
    return rmsnorm(x, final_norm)


def setup_inputs(seed: int = 0) -> dict:
    key = jax.random.key(seed)
    ks = jax.random.split(key, 16)
    f32 = jnp.float32

    def gain(k, shape):
        return 1.0 + 0.02 * jax.random.normal(k, shape, f32)

    return {
        "x_prompt": jax.random.normal(ks[0], (BATCH, SEQ, D_MODEL), f32),
        "x_sample": jax.random.normal(ks[1], (DEC_BATCH, DEC_SEQ, D_MODEL), f32),
        "w_in": jax.random.normal(ks[2], (DEPTH, D_MODEL, IN_WIDTH), f32) * D_MODEL ** -0.5,
        "q_norm": gain(ks[3], (DEPTH, HEAD_DIM)),
        "k_norm": gain(ks[4], (DEPTH, HEAD_DIM)),
        "conv_w": jax.random.normal(ks[5], (DEPTH, CONV_K, CONV_WIDTH), f32) * CONV_K ** -0.5,
        "attn_grp_norm": gain(ks[6], (DEPTH, ATTN_WIDTH)),
        "conv_grp_norm": gain(ks[7], (DEPTH, CONV_WIDTH)),
        "w_out": jax.random.normal(ks[8], (DEPTH, D_MODEL, D_MODEL), f32) * D_MODEL ** -0.5,
        "mix_norm": gain(ks[9], (DEPTH, D_MODEL)),
        "mlp_norm": gain(ks[10], (DEPTH, D_MODEL)),
        "w_up": jax.random.normal(ks[11], (DEPTH, D_MODEL, D_FF), f32) * D_MODEL ** -0.5,
        "w_down": jax.random.normal(ks[12], (DEPTH, D_FF, D_MODEL), f32) * D_FF ** -0.5,
        "final_norm": gain(ks[13], (D_MODEL,)),
    }


def reference(x_prompt, x_sample, w_in, q_norm, k_norm, conv_w, attn_grp_norm,
              conv_grp_norm, w_out, mix_norm, mlp_norm, w_up, w_down, final_norm):
    y_prompt = run_trunk(x_prompt, w_in, q_norm, k_norm, conv_w, attn_grp_norm,
                         conv_grp_norm, w_out, mix_norm, mlp_norm, w_up, w_down, final_norm)
    y_sample = run_trunk(x_sample, w_in, q_norm, k_norm, conv_w, attn_grp_norm,
                         conv_grp_norm, w_out, mix_norm, mlp_norm, w_up, w_down, final_norm)
    return (y_prompt, y_sample)
```

```python
from contextlib import ExitStack
import numpy as np
import concourse.bass as bass
import concourse.mybir as mybir
from concourse.bass_utils import run_bass_kernel_spmd

F32 = mybir.dt.float32
BF16 = mybir.dt.bfloat16
AF = mybir.ActivationFunctionType
ALU = mybir.AluOpType
AX = mybir.AxisListType

EPS = 1e-6
GRID_W = 64
ROPE_THETA = 10000.0
T = 512
TB = 4
HCK = 8
NWB = 4


class Cfg:
    def __init__(self, D, NH, NKV, CG, DFF, S, nA, nB):
        self.D, self.NH, self.NKV, self.CG, self.DFF, self.S, self.nA, self.nB = D, NH, NKV, CG, DFF, S, nA, nB
        self.KC = D // 128
        self.AW = NH * 128
        self.KVW = NKV * 128
        self.CWD = CG * 128
        assert self.AW + self.CWD == D
        self.IN = self.AW + 2 * self.KVW + 3 * self.CWD
        self.FC = DFF // 128
        self.NCH = self.FC // HCK
        self.MK = min(8, self.KC)
        self.CW = min(512, D)
        self.NCG = D // self.CW
        self.NT = S // T
        self.KT = S // 128
        self.WE = max(self.KC * 128, self.MK * max(self.CW, self.KVW), HCK * self.CW)
        self.G = NH // NKV
        self.oq, self.ok, self.ov = 0, self.AW, self.AW + self.KVW
        self.oB = self.AW + 2 * self.KVW
        self.oC = self.oB + self.CWD
        self.ou = self.oC + self.CWD


FULL = Cfg(D=4096, NH=16, NKV=4, CG=16, DFF=16384, S=4096, nA=4, nB=2)


class Sem:
    def __init__(self, nc, es, name):
        self.h = es.enter_context(nc.semaphore(name))
        self.n = 0
        self.name = name


class Engine:
    def __init__(self, name, eng, sem):
        self.name, self.eng, self.sem = name, eng, sem
        self.waited = {}


class Buf:
    def __init__(self, name):
        self.name = name
        self.w = None
        self.r = {}


class Tracker:
    def __init__(self):
        self.engines = []

    def _waits(self, E, reads, writes, is_dma, same_ok=False):
        raw, war = {}, {}

        def add(d, tok):
            if tok is None:
                return
            sem, val = tok
            if d.get(sem, 0) < val:
                d[sem] = val
        for b in reads:
            add(raw, b.w)
        for b in writes:
            add(raw, b.w)
            for sem, val in b.r.items():
                add(war, (sem, val))
        need = dict(raw)
        for sem, val in war.items():
            if sem is E.sem and not is_dma:
                continue
            if need.get(sem, 0) < val:
                need[sem] = val
        for sem, val in need.items():
            if sem is E.sem and not is_dma and (same_ok or E.name in ("pe", "sp")):
                continue
            if E.waited.get(sem, 0) >= val:
                continue
            E.eng.wait_ge(sem.h, val)
            E.waited[sem] = val

    def _commit(self, tok, reads, writes):
        for b in writes:
            b.w = tok
            b.r = {}
        for b in reads:
            sem, val = tok
            if b.r.get(sem, 0) < val:
                b.r[sem] = val

    def op(self, E, fn, reads=(), writes=(), same_ok=False):
        self._waits(E, reads, writes, False, same_ok)
        ins = fn(E.eng)
        E.sem.n += 1
        ins.then_inc(E.sem.h, 1)
        tok = (E.sem, E.sem.n)
        self._commit(tok, reads, writes)
        return tok

    def dma(self, E, dsem, out_ap, in_ap, reads=(), writes=()):
        self._waits(E, reads, writes, True)
        ins = E.eng.dma_start(out=out_ap, in_=in_ap)
        dsem.n += 1
        ins.then_inc(dsem.h, 16)
        tok = (dsem, 16 * dsem.n)
        self._commit(tok, reads, writes)
        return tok

    def wait_tok(self, E, tok):
        if tok is None:
            return
        sem, val = tok
        if E.waited.get(sem, 0) >= val:
            return
        E.eng.wait_ge(sem.h, val)
        E.waited[sem] = val


def transfer(srcs, dsts):
    for d in dsts:
        for s in srcs:
            for tok in ([s.w] if s.w is not None else []) + list(s.r.items()):
                sem, val = tok
                if d.r.get(sem, 0) < val:
                    d.r[sem] = val


def build(cfg, debug_stage=None):
    c = cfg
    D, KC, NH, NKV, CG, S = c.D, c.KC, c.NH, c.NKV, c.CG, c.S
    nc = bass.Bass("TRN2", target_bir_lowering=False)

    def din(name, shape, dt=F32):
        return nc.dram_tensor(name, list(shape), dt, kind="ExternalInput").ap()

    def dscr(name, shape, dt=BF16):
        return nc.dram_tensor(name, list(shape), dt, kind="Internal").ap()

    xs = [din("xA", [S, D]), din("xB", [S, D])]
    nown = [c.nA, c.nB]
    NOWN = c.nA + c.nB
    xh_d = din("xh", [NOWN, 128, 2 * KC])
    ctab_d = din("ctab", [2 * c.NT, 128, T])
    stab_d = din("stab", [2 * c.NT, 128, T])
    w_in_d = din("w_in", [D, c.IN])
    w_out_d = din("w_out", [D, D])
    w_up_d = din("w_up", [D, c.DFF])
    w_down_d = din("w_down", [c.DFF, D])
    gmix_d = din("gmix", [128, KC])
    gmlp_d = din("gmlp", [128, KC])
    gq_d = din("gq", [128, 1])
    gk_d = din("gk", [128, 1])
    convw_d = din("convw", [128, CG * 3])
    ga_d = din("ga", [128, NH])
    gc_d = din("gc", [128, CG])
    gfin_d = din("gfin", [128, D])
    ident_d = din("ident", [128, 128])
    rm_d = din("rm", [128, 128])
    y_d = nc.dram_tensor("y", [NOWN * T, D], F32, kind="ExternalOutput").ap()

    n_s_in = (c.AW + c.KVW + 3 * c.CWD) // 128
    ws_in = dscr("ws_in", [n_s_in, 128, KC * 128])
    NVT = KC // c.MK
    ws_v = dscr("ws_v", [NVT, 128, c.MK * c.KVW])
    NOT = KC // c.MK
    ws_out = dscr("ws_out", [c.NCG * NOT, 128, c.MK * c.CW])
    ws_up = dscr("ws_up", [c.FC, 128, KC * 128])
    ws_dn = dscr("ws_dn", [c.NCH * c.NCG, 128, HCK * c.CW])
    kS = dscr("kS", [2 * NKV, 128, S])
    vS = dscr("vS", [2 * NKV, 128, c.KT * 128])
    qS = dscr("qS", [NOWN, 128, NH * T])
    cS = dscr("cS", [NOWN, 128, CG * T])

    def s_in_index(col):
        cb = col // 128
        if col >= c.ov + c.KVW:
            cb -= c.KVW // 128
        return cb

    tr = Tracker()
    with ExitStack() as es:
        E_ = es.enter_context

        def sb(name, shape, dt):
            return E_(nc.sbuf_tensor(name, list(shape), dt))

        xres = sb("xres", [128, TB, D], F32)
        hT = sb("hT", [128, KC, T], BF16)
        acT = sb("acT", [128, KC, T], BF16)
        qT = sb("qT", [128, NH * T], BF16)
        assert NH * T * 2 >= D * 4
        wb = [sb(f"wb{i}", [128, c.WE], BF16) for i in range(NWB)]
        NWK = 8
        wk = [sb(f"wk{i}", [128, T + 2], F32) for i in range(NWK)]
        NPT = 4
        pTb = [sb(f"pT{i}", [128, T], BF16) for i in range(NPT)]
        hp = [sb(f"hp{i}", [128, 512], BF16) for i in range(2)]
        hTh = sb("hTh", [128, KC * 2], BF16)
        xhs = sb("xhs", [128, 2 * KC], F32)
        xhq = sb("xhq", [128, 2 * KC], F32)
        ctab = sb("ctab_sb", [128, T], F32)
        stab = sb("stab_sb", [128, T], F32)
        stat = sb("stat", [128, 64], F32)
        st2 = sb("st2", [128, 8], F32)
        rstd = sb("rstd", [128, 8], F32)
        uh = sb("uh", [128, 4], F32)
        ident_f = sb("ident_f", [128, 128], F32)
        ident = sb("ident_b", [128, 128], BF16)
        rm = sb("rm_sb", [128, 128], F32)
        onesm = sb("onesm", [128, 128], F32)
        ones_b = sb("ones_b", [128, 128], BF16)
        eps_t = sb("eps_t", [128, 1], F32)
        gmix = sb("gmix_sb", [128, KC], F32)
        gmlp = sb("gmlp_sb", [128, KC], F32)
        gq = sb("gq_sb", [128, 1], F32)
        gk = sb("gk_sb", [128, 1], F32)
        convw = sb("convw_sb", [128, CG * 3], F32)
        ga = sb("ga_sb", [128, NH], F32)
        gc = sb("gc_sb", [128, CG], F32)

        hT_flat = hT[:].rearrange("p k t -> p (k t)")
        kvsz = S + c.KT * 128
        assert 2 * kvsz <= KC * T, "kv double buffer must fit in hT"
        acT_flat = acT[:].rearrange("p k t -> p (k t)")
        assert 2 * HCK * T <= KC * T
        gfin = qT[:].bitcast(F32)

        def kT_view(i):
            return hT_flat[:, i * kvsz: i * kvsz + S]

        def v_view(i):
            return hT_flat[:, i * kvsz + S: (i + 1) * kvsz]

        def act_view(i):
            return acT_flat[:, i * HCK * T: (i + 1) * HCK * T]

        ps = [E_(nc.psum_tensor(f"ps{i}", [128, 512], F32)) for i in range(8)]
        ps7b = ps[7][:].bitcast(BF16)
        ps6b = ps[6][:].bitcast(BF16)
        ptb = [ps6b, ps7b]

        def mk(name):
            return Sem(nc, es, name)
        PE = Engine("pe", nc.tensor, mk("s_pe"))
        ACT = Engine("act", nc.scalar, mk("s_act"))
        DVE = Engine("dve", nc.vector, mk("s_dve"))
        POOL = Engine("pool", nc.gpsimd, mk("s_pool"))
        SP = Engine("sp", nc.sync, mk("s_sp"))
        d_w = [mk(f"d_w{i}") for i in range(NWB)]
        d_x = [mk(f"d_x{i}") for i in range(TB)]
        d_xh, d_ct, d_st = mk("d_xh"), mk("d_ct"), mk("d_st")
        d_kv = [mk("d_kv0"), mk("d_kv1")]
        d_gf = mk("d_gf")
        d_qs, d_cs = mk("d_qs"), mk("d_cs")
        d_y = [mk(f"d_y{i}") for i in range(TB)]
        d_ks = [mk("d_ks0"), mk("d_ks1")]
        d_vs = [mk(f"d_vs{i}") for i in range(4)]
        d_c = [mk(f"d_c{i}") for i in range(12)]
        d_p0l = [mk(f"d_p0l{i}") for i in range(4)]
        d_p0s = [mk(f"d_p0s{i}") for i in range(4)]

        B_xr = [Buf(f"xres{i}") for i in range(TB)]
        B_hT, B_acT, B_qT = Buf("hT"), Buf("acT"), Buf("qT")
        B_kv = [Buf("kv0"), Buf("kv1")]
        B_act = [Buf("act0"), Buf("act1")]
        B_gfin = Buf("gfin")
        B_wb = [Buf(f"wb{i}") for i in range(NWB)]
        B_wk = [Buf(f"wk{i}") for i in range(NWK)]
        B_pT = [Buf(f"pT{i}") for i in range(NPT)]
        B_hp = [Buf("hp0"), Buf("hp1")]
        B_hTh, B_xhq, B_xhal = Buf("hTh"), Buf("xhq"), Buf("xhal")
        B_ctab, B_stab = Buf("ctab"), Buf("stab")
        kst = [pTb[0], pTb[1]]
        B_kst = [B_pT[0], B_pT[1]]
        B_stat, B_st2, B_rstd, B_uh = Buf("stat"), Buf("st2"), Buf("rstd"), Buf("uh")
        B_ps = [Buf(f"ps{i}") for i in range(8)]
        B_p7h = [Buf("p7h0"), Buf("p7h1")]
        B_const = Buf("const")
        B_wsin, B_wsv, B_wsout, B_wsup, B_wsdn = Buf("wsin"), Buf("wsv"), Buf("wsout"), Buf("wsup"), Buf("wsdn")
        B_kS, B_vS = [Buf("kS0"), Buf("kS1")], [Buf("vS0"), Buf("vS1")]
        B_y = Buf("y")
        store_toks = {}
        kv_toks = {}
        y_toks = {}

        def note_store(tok, d=None):
            d = store_toks if d is None else d
            d[tok[0]] = tok[1]

        small = [(gmix, gmix_d), (gmlp, gmlp_d), (gq, gq_d), (gk, gk_d), (convw, convw_d), (ga, ga_d),
                 (gc, gc_d), (ident_f, ident_d), (rm, rm_d)]
        for i, (t_sb, t_d) in enumerate(small):
            tr.dma(SP, d_c[i], t_sb[:], t_d, writes=[B_const])
        tr.op(POOL, lambda e: e.memset(onesm[:], 1.0 / 128.0), writes=[B_const])
        tr.op(POOL, lambda e: e.memset(ones_b[:], 1.0), writes=[B_const])
        tr.op(POOL, lambda e: e.memset(eps_t[:], EPS), writes=[B_const])
        for i in range(len(small)):
            tr.wait_tok(DVE, (d_c[i], 16))
        tr.op(DVE, lambda e: e.tensor_copy(out=ident[:], in_=ident_f[:]), reads=[B_const], writes=[B_const])
        for E in (PE, ACT, POOL):
            for i in range(len(small)):
                tr.wait_tok(E, (d_c[i], 16))
            tr.wait_tok(E, B_const.w)
            tr.wait_tok(E, (POOL.sem, POOL.sem.n))
        tr.wait_tok(DVE, (POOL.sem, POOL.sem.n))
        B_const.w = None
        B_const.r = {}
        gqk = sb("gqk", [128, 2], F32)
        gmx = sb("gmx", [2, 4], F32)
        negc = sb("negc", [128, 1], F32)
        ones_f = sb("ones_f", [128, 128], F32)
        B_sm = Buf("softmax_shift")
        tr.op(ACT, lambda e: e.activation(out=gqk[:, 0:1], in_=gq[:, 0:1], func=AF.Abs), writes=[B_sm])
        tr.op(ACT, lambda e: e.activation(out=gqk[:, 1:2], in_=gk[:, 0:1], func=AF.Abs), reads=[B_sm], writes=[B_sm])
        tr.op(DVE, lambda e: e.memset(ones_f[:], 1.0), reads=[B_sm], writes=[B_sm])
        tr.op(PE, lambda e: e.matmul(ps[0][0:2, 0:128], lhsT=gqk[:, 0:2], rhs=ident_f[:], start=True, stop=True),
              reads=[B_sm], writes=[B_ps[0]])
        tr.op(DVE, lambda e: e.tensor_reduce(out=gmx[0:2, 0:1], in_=ps[0][0:2, 0:128], axis=AX.X, op=ALU.max),
              reads=[B_ps[0]], writes=[B_sm])
        tr.op(DVE, lambda e: e.tensor_scalar(out=gmx[0:2, 2:4], in0=ident_f[0:2, 0:2], scalar1=gmx[0:2, 0:1], scalar2=None,
                                             op0=ALU.mult), reads=[B_sm], writes=[B_sm])
        tr.op(PE, lambda e: e.matmul(ps[1][:, 0:2], lhsT=ones_f[0:2, :], rhs=gmx[0:2, 2:4], start=True, stop=True),
              reads=[B_sm], writes=[B_ps[1]])
        tr.op(DVE, lambda e: e.tensor_copy(out=gqk[:, 0:2], in_=ps[1][:, 0:2]), reads=[B_ps[1]], writes=[B_sm])
        tr.op(DVE, lambda e: e.scalar_tensor_tensor(out=negc[:, 0:1], in0=gqk[:, 0:1], scalar=-(128.0 ** 0.5),
                                                    in1=gqk[:, 1:2], op0=ALU.mult, op1=ALU.mult),
              reads=[B_sm], writes=[B_sm])
        ones32 = ones_f
        tr.op(DVE, lambda e: e.memset(ones32[:], 1.0 / 32.0), reads=[B_sm, B_ps[1]], writes=[B_sm])
        tr.wait_tok(ACT, B_sm.w)
        tr.wait_tok(PE, B_sm.w)

        KG = 4
        NS = 4
        acT_f32 = acT_flat.bitcast(F32)
        if min((KC * T // 2) // (NS * KG), (NH * T) // (NS * KG)) < c.CW:
            KG = 2
        PW = min(512, (KC * T // 2) // (NS * KG), (NH * T) // (NS * KG))

        def stage_f(i):
            return acT_f32[:, i * KG * PW:(i + 1) * KG * PW]

        def stage_b(i):
            return qT[:, i * KG * PW:(i + 1) * KG * PW]
        B_sf = [Buf(f"sf{i}") for i in range(NS)]
        B_sb = [Buf(f"sb{i}") for i in range(NS)]
        p0_step = [0]
        cast_rr = [0]

        def p0(w_d, row0, nk, col0, ncols, uw, kgmax, gain, dst_fn):
            nkc = nk // 128
            kg = min(KG, nkc, kgmax)
            assert PW % uw == 0 and ncols % uw == 0
            for c0 in range(0, ncols, PW):
                pw = min(PW, ncols - c0)
                nu = pw // uw
                for kc0 in range(0, nkc, kg):
                    i = p0_step[0] % NS
                    p0_step[0] += 1
                    r0 = row0 + kc0 * 128
                    sfv = stage_f(i)[:, 0:kg * pw].rearrange("p (k c) -> p k c", k=kg)

                    def load_fn(i=i, sfv=sfv, r0=r0, c0=c0, pw=pw):
                        tr.dma(ACT, d_p0l[i], sfv,
                               w_d[r0:r0 + kg * 128, col0 + c0:col0 + c0 + pw].rearrange("(k p) c -> p k c", p=128),
                               writes=[B_sf[i]])

                    def work_fn(i=i, sfv=sfv, c0=c0, pw=pw, nu=nu, kc0=kc0):
                        sbv = stage_b(i)[:, 0:kg * pw].rearrange("p (u k j) -> p u k j", u=nu, k=kg)
                        for k in range(kg):
                            src = sfv[:, k, :].rearrange("p (u j) -> p u j", u=nu)
                            dst = sbv[:, :, k, :]
                            which = 1 if (cast_rr[0] % 4 == 3) else 0
                            cast_rr[0] += 1
                            if gain is None:
                                if which == 0:
                                    tr.op(DVE, lambda e: e.tensor_copy(out=dst, in_=src), reads=[B_sf[i]],
                                          writes=[B_sb[i]], same_ok=True)
                                else:
                                    tr.op(ACT, lambda e: e.activation(out=dst, in_=src, func=AF.Copy),
                                          reads=[B_sf[i]], writes=[B_sb[i]], same_ok=True)
                            else:
                                gk_ = row0 // 128 + kc0 + k
                                g_ap = gain[:, gk_:gk_ + 1]
                                if which == 0:
                                    tr.op(DVE, lambda e: e.tensor_scalar(out=dst, in0=src, scalar1=g_ap, scalar2=None,
                                                                         op0=ALU.mult),
                                          reads=[B_sf[i]], writes=[B_sb[i]], same_ok=True)
                                else:
                                    tr.op(ACT, lambda e: e.activation(out=dst, in_=src, func=AF.Copy, scale=g_ap),
                                          reads=[B_sf[i]], writes=[B_sb[i]], same_ok=True)
                        for E in (DVE, ACT):
                            tr.wait_tok(POOL, (E.sem, E.sem.n))
                        sb3 = stage_b(i)[:, 0:kg * pw].rearrange("p (u x) -> p u x", u=nu)
                        for dram_ap, sb_ap in dst_fn(c0 // uw, nu, kc0, kg, sb3):
                            tok = tr.dma(POOL, d_p0s[i], dram_ap, sb_ap, reads=[B_sb[i]], writes=[])
                            note_store(tok)
                    yield (load_fn, work_fn)

        class StepPipe:
            def __init__(self, gen):
                self.gen = gen
                self.q = []
                self.done = False

            def _fill(self):
                while not self.done and len(self.q) < NS:
                    st = next(self.gen, None)
                    if st is None:
                        self.done = True
                        break
                    st[0]()
                    self.q.append(st[1])

            def step(self):
                self._fill()
                if not self.q:
                    return False
                self.q.pop(0)()
                self._fill()
                return True

        def conv_S(w_d, col0, ncols, gain, ws, tile0):
            def dst_fn(u0, nu, kc0, kg, sb3):
                t0 = tile0 + u0
                return [(ws[t0:t0 + nu, :, kc0 * 128:(kc0 + kg) * 128].rearrange("c p x -> p c x"), sb3)]
            yield from p0(w_d, 0, D, col0, ncols, 128, KG, gain, dst_fn)

        def conv_M(w_d, row0, nk, col0, ncols, cw, mk, gain, ws, tile_fn):
            def dst_fn(u0, nu, kc0, kg, sb3):
                kt, kin = kc0 // mk, kc0 % mk
                assert kin + kg <= mk
                return [(ws[tile_fn(u0 + n, kt), :, kin * cw:(kin + kg) * cw], sb3[:, n, :]) for n in range(nu)]
            assert mk % min(KG, mk) == 0
            yield from p0(w_d, row0, nk, col0, ncols, cw, mk, gain, dst_fn)

        def p0_kv():
            yield from conv_S(w_in_d, c.ok, c.KVW, gmix, ws_in, c.ok // 128)
            yield from conv_M(w_in_d, 0, D, c.ov, c.KVW, c.KVW, c.MK, gmix, ws_v, lambda n, kt: kt)

        def p0_rest():
            yield from conv_S(w_in_d, 0, c.AW, gmix, ws_in, 0)
            yield from conv_S(w_in_d, c.oB, 3 * c.CWD, gmix, ws_in, (c.AW + c.KVW) // 128)
            yield from conv_M(w_out_d, 0, D, 0, D, c.CW, c.MK, None, ws_out, lambda n, kt: n * NOT + kt)
            yield from conv_S(w_up_d, 0, c.DFF, gmlp, ws_up, 0)
            for j in range(c.NCH):
                yield from conv_M(w_down_d, j * HCK * 128, HCK * 128, 0, D, c.CW, HCK, None, ws_dn,
                                  lambda n, kt, j=j: j * c.NCG + n)

        kvpipe = StepPipe(p0_kv())
        while kvpipe.step():
            pass
        for sem, val in store_toks.items():
            tr.wait_tok(SP, (sem, val))
        restpipe = StepPipe(p0_rest())

        def nsteps(ncols, nk, kgmax):
            kg = min(KG, nk // 128, kgmax)
            return -(-ncols // PW) * ((nk // 128) // kg)
        n_rest = (nsteps(c.AW, D, KG) + nsteps(3 * c.CWD, D, KG) + nsteps(D, D, c.MK) + nsteps(c.DFF, D, KG)
                  + c.NCH * nsteps(D, HCK * 128, HCK))
        n_kv_tiles = sum(c.NT for sl in range(2) if nown[sl] > 0)
        per_tile = -(-n_rest // max(1, n_kv_tiles))
        p_head = per_tile // (NKV + 3)
        pump_n = [0, p_head, per_tile - p_head * NKV]

        pumped = {"n": 0}
        hk = {"rate": 0.0, "acc": 0.0}

        def hook():
            hk["acc"] += hk["rate"]
            while hk["acc"] >= 1.0:
                hk["acc"] -= 1.0
                pump(1)

        def pump(n):
            for _ in range(n):
                if not restpipe.step():
                    return
                pumped["n"] += 1

        def p0_barrier():
            pump(1 << 30)
            for E in (SP, PE, ACT, DVE, POOL):
                for sem, val in store_toks.items():
                    tr.wait_tok(E, (sem, val))
                for E2 in (ACT, DVE, POOL):
                    if E2 is not E:
                        tr.wait_tok(E, (E2.sem, E2.sem.n))
            store_toks.clear()


        wring = {"n": 0}

        def wload(src_ap, nelem):
            i = wring["n"] % NWB
            wring["n"] += 1
            tr.dma(SP, d_w[i], wb[i][:, 0:nelem], src_ap, writes=[B_wb[i]])
            return i

        wk_rr = {"n": 0}

        def getwk():
            i = wk_rr["n"] % NWK
            wk_rr["n"] += 1
            return i

        NPC = D // 512 if D >= 512 else 1

        def sq_piece(tb, pc):
            j = getwk()
            tr.op(ACT, lambda e: e.activation(out=wk[j][:, 0:512], in_=xres[:, tb, pc * 512:(pc + 1) * 512],
                                              func=AF.Square, accum_out=stat[:, tb * NPC + pc:tb * NPC + pc + 1]),
                  reads=[B_xr[tb], B_stat], writes=[B_wk[j]])

        def stats_reset():
            tr.op(DVE, lambda e: e.memset(stat[:], 0.0), writes=[B_stat])

        def stats_finish():
            tr.wait_tok(DVE, (ACT.sem, ACT.sem.n))
            tr.op(DVE, lambda e: e.tensor_reduce(out=st2[:, 0:TB], in_=stat[:, 0:TB * NPC].rearrange("p (a b) -> p a b", a=TB),
                                                 axis=AX.X, op=ALU.add), reads=[B_stat], writes=[B_st2])
            tr.op(ACT, lambda e: e.activation(out=st2[:, 0:TB], in_=st2[:, 0:TB], func=AF.Sqrt, bias=eps_t[:, 0:1],
                                              scale=1.0 / D), reads=[B_st2], writes=[B_st2])
            tr.op(DVE, lambda e: e.reciprocal(out=rstd[:, 0:TB], in_=st2[:, 0:TB]), reads=[B_st2], writes=[B_rstd])

        def rms_prologue(x_src, halo_src, tab_idx, stats_done=False):
            if x_src is not None:
                for tb in range(TB):
                    tr.dma(SP, d_x[tb], xres[:, tb, :], x_src[tb * 128:(tb + 1) * 128, :], writes=[B_xr[tb]])
            if tab_idx is not None:
                tr.dma(SP, d_ct, ctab[:], ctab_d[tab_idx], writes=[B_ctab])
                tr.dma(SP, d_st, stab[:], stab_d[tab_idx], writes=[B_stab])
            if halo_src is not None:
                tr.dma(SP, d_xh, xhs[:], halo_src, writes=[B_xhal])
            if not stats_done:
                stats_reset()
                for tb in range(TB):
                    for pc in range(NPC):
                        sq_piece(tb, pc)
            stats_finish()
            it = 0
            for tb in range(TB):
                for g0 in range(0, KC, 4):
                    ng = min(4, KC - g0)
                    i = it % 2
                    it += 1
                    tr.op(DVE, lambda e: e.tensor_scalar(out=hp[i][:, 0:ng * 128], in0=xres[:, tb, g0 * 128:(g0 + ng) * 128],
                                                         scalar1=rstd[:, tb:tb + 1], scalar2=None, op0=ALU.mult),
                          reads=[B_xr[tb], B_rstd], writes=[B_hp[i]])

                    def tp(e, i=i, ng=ng):
                        ins = None
                        for k in range(ng):
                            ins = e.transpose(ptb[i][:, k * 128:(k + 1) * 128],
                                              hp[i][:, k * 128:(k + 1) * 128], ident[:])
                        return ins
                    B_pst = B_ps[6 + i]
                    tr.op(PE, tp, reads=[B_hp[i]], writes=[B_pst])
                    src = ptb[i][:, 0:ng * 128].rearrange("p (k t) -> p k t", k=ng)
                    dst = hT[:, g0:g0 + ng, tb * 128:(tb + 1) * 128]
                    tr.op(ACT, lambda e: e.activation(out=dst, in_=src, func=AF.Copy), reads=[B_pst], writes=[B_hT],
                          same_ok=True)
                    hook()
            if halo_src is not None:
                tr.op(DVE, lambda e: e.tensor_tensor(out=xhq[:], in0=xhs[:], in1=xhs[:], op=ALU.mult),
                      reads=[B_xhal], writes=[B_xhq])
                tr.op(DVE, lambda e: e.tensor_reduce(out=st2[:, 4:6], in_=xhq[:].rearrange("p (r k) -> p r k", r=2),
                                                     axis=AX.X, op=ALU.add), reads=[B_xhq], writes=[B_st2])
                tr.op(PE, lambda e: e.matmul(ps[5][:, 0:2], lhsT=onesm[:], rhs=st2[:, 4:6], start=True, stop=True),
                      reads=[B_st2], writes=[B_ps[5]])
                tr.op(ACT, lambda e: e.activation(out=st2[:, 6:8], in_=ps[5][:, 0:2], func=AF.Sqrt, bias=eps_t[:, 0:1],
                                                  scale=1.0 / KC), reads=[B_ps[5]], writes=[B_st2])
                tr.op(DVE, lambda e: e.reciprocal(out=rstd[:, 4:6], in_=st2[:, 6:8]), reads=[B_st2], writes=[B_rstd])
                hTh_v = hTh[:].rearrange("p (k r) -> p r k", r=2)
                for r in range(2):
                    tr.op(DVE, lambda e, r=r: e.tensor_scalar(out=hTh_v[:, r, :], in0=xhs[:, r * KC:(r + 1) * KC],
                                                              scalar1=rstd[:, 4 + r:5 + r], scalar2=None, op0=ALU.mult),
                          reads=[B_xhal, B_rstd], writes=[B_hTh])

        def proj_S(slot, pbank, halo_bank_ap=None):
            def f(e):
                ins = None
                for kc in range(KC):
                    ins = e.matmul(ps[pbank][:, 0:T], lhsT=wb[slot][:, kc * 128:(kc + 1) * 128], rhs=hT[:, kc, :],
                                   start=(kc == 0), stop=(kc == KC - 1))
                if halo_bank_ap is not None:
                    for kc in range(KC):
                        ins = e.matmul(halo_bank_ap, lhsT=wb[slot][:, kc * 128:(kc + 1) * 128],
                                       rhs=hTh[:, kc * 2:(kc + 1) * 2], start=(kc == 0), stop=(kc == KC - 1))
                return ins
            rd = [B_wb[slot], B_hT] + ([B_hTh] if halo_bank_ap is not None else [])
            wr = [B_ps[pbank]] + ([B_ps[6]] if halo_bank_ap is not None else [])
            tr.op(PE, f, reads=rd, writes=wr)

        pend = {"f": None}

        def defer(fn):
            prev = pend["f"]
            pend["f"] = fn
            if prev is not None:
                prev()

        def flush():
            defer(None)

        def qk_head(slot, bset, gain_ap, out_ap, out_buf, post=None, tail_eng=None):
            bq, bs, br = bset
            proj_S(slot, bq)
            i_sq, i_qg, i_t2 = getwk(), getwk(), getwk()
            tr.op(ACT, lambda e: e.activation(out=wk[i_sq][:, 0:T], in_=ps[bq][:, 0:T], func=AF.Square),
                  reads=[B_ps[bq]], writes=[B_wk[i_sq]])
            tr.op(ACT, lambda e: e.activation(out=wk[i_qg][:, 0:T], in_=ps[bq][:, 0:T], func=AF.Copy, scale=gain_ap),
                  reads=[B_ps[bq]], writes=[B_wk[i_qg]])
            def tail():
                qk_tail(bs, br, i_sq, i_qg, i_t2, out_ap, out_buf, tail_eng or POOL)
                if post is not None:
                    post()
            defer(tail)

        def qk_tail(bs, br, i_sq, i_qg, i_t2, out_ap, out_buf, TE):
            tr.op(PE, lambda e: e.matmul(ps[bs][:, 0:T], lhsT=onesm[:], rhs=wk[i_sq][:, 0:T], start=True, stop=True),
                  reads=[B_wk[i_sq]], writes=[B_ps[bs]])
            tr.op(PE, lambda e: e.matmul(ps[br][:, 0:T], lhsT=rm[:], rhs=wk[i_qg][:, 0:T], start=True, stop=True),
                  reads=[B_wk[i_qg]], writes=[B_ps[br]])
            tr.op(ACT, lambda e: e.activation(out=wk[i_sq][:, 0:T], in_=ps[bs][:, 0:T], func=AF.Sqrt, bias=eps_t[:, 0:1],
                                              scale=1.0), reads=[B_ps[bs]], writes=[B_wk[i_sq]])
            tr.op(DVE, lambda e: e.reciprocal(out=wk[i_sq][:, 0:T], in_=wk[i_sq][:, 0:T]),
                  reads=[B_wk[i_sq]], writes=[B_wk[i_sq]])
            tr.op(DVE, lambda e: e.tensor_tensor(out=wk[i_t2][:, 0:T], in0=ps[br][:, 0:T], in1=stab[:], op=ALU.mult),
                  reads=[B_ps[br], B_stab], writes=[B_wk[i_t2]])
            tr.op(TE, lambda e: e.tensor_tensor(out=wk[i_qg][:, 0:T], in0=wk[i_qg][:, 0:T], in1=ctab[:], op=ALU.mult),
                  reads=[B_wk[i_qg], B_ctab], writes=[B_wk[i_qg]])
            tr.op(TE, lambda e: e.tensor_tensor(out=wk[i_qg][:, 0:T], in0=wk[i_qg][:, 0:T], in1=wk[i_t2][:, 0:T],
                                                  op=ALU.add), reads=[B_wk[i_qg], B_wk[i_t2]], writes=[B_wk[i_qg]])
            tr.op(TE, lambda e: e.tensor_tensor(out=out_ap, in0=wk[i_qg][:, 0:T], in1=wk[i_sq][:, 0:T], op=ALU.mult),
                  reads=[B_wk[i_qg], B_wk[i_sq]], writes=[out_buf])

        st_rr = {"n": 0}

        def stage_out():
            j = st_rr["n"] % NPT
            st_rr["n"] += 1
            return pTb[j], B_pT[j], d_vs[j]

        def conv_group(g, out_ap, out_buf, post):
            bB, bC, bU = (0, 1, 2) if g % 2 == 0 else (3, 4, 5)
            hb = (g % 2) * 8
            sB = wload(ws_in[s_in_index(c.oB + g * 128)], KC * 128)
            proj_S(sB, bB)
            hook()
            sC = wload(ws_in[s_in_index(c.oC + g * 128)], KC * 128)
            proj_S(sC, bC, ps[6][:, hb:hb + 2])
            hook()
            sU = wload(ws_in[s_in_index(c.ou + g * 128)], KC * 128)
            proj_S(sU, bU, ps[6][:, hb + 2:hb + 4])
            i_u, i_cu, i_y, i_sq = getwk(), getwk(), getwk(), getwk()
            tr.op(ACT, lambda e: e.activation(out=wk[i_u][:, 0:T], in_=ps[bU][:, 0:T], func=AF.Copy),
                  reads=[B_ps[bU]], writes=[B_wk[i_u]])
            tr.op(ACT, lambda e: e.activation(out=uh[:, 0:2], in_=ps[6][:, hb + 2:hb + 4], func=AF.Copy),
                  reads=[B_ps[6]], writes=[B_uh])
            tr.op(DVE, lambda e: e.tensor_tensor(out=wk[i_cu][:, 1:T + 1], in0=ps[bC][:, 0:T], in1=wk[i_u][:, 0:T],
                                                 op=ALU.mult), reads=[B_ps[bC], B_wk[i_u]], writes=[B_wk[i_cu]])
            tr.op(DVE, lambda e: e.tensor_tensor(out=wk[i_cu][:, 0:1], in0=ps[6][:, hb:hb + 1], in1=uh[:, 0:1],
                                                 op=ALU.mult), reads=[B_ps[6], B_uh], writes=[B_wk[i_cu]])
            tr.op(DVE, lambda e: e.tensor_tensor(out=wk[i_cu][:, T + 1:T + 2], in0=ps[6][:, hb + 1:hb + 2],
                                                 in1=uh[:, 1:2], op=ALU.mult),
                  reads=[B_ps[6], B_uh], writes=[B_wk[i_cu]])
            tr.op(DVE, lambda e: e.tensor_scalar(out=wk[i_y][:, 0:T], in0=wk[i_cu][:, 0:T],
                                                 scalar1=convw[:, g * 3:g * 3 + 1], scalar2=None, op0=ALU.mult),
                  reads=[B_wk[i_cu]], writes=[B_wk[i_y]])
            for jw in (1, 2):
                tr.op(DVE, lambda e, jw=jw: e.scalar_tensor_tensor(out=wk[i_y][:, 0:T], in0=wk[i_cu][:, jw:jw + T],
                                                                   scalar=convw[:, g * 3 + jw:g * 3 + jw + 1],
                                                                   in1=wk[i_y][:, 0:T], op0=ALU.mult, op1=ALU.add),
                      reads=[B_wk[i_cu], B_wk[i_y]], writes=[B_wk[i_y]])
            tr.op(DVE, lambda e: e.tensor_tensor(out=wk[i_y][:, 0:T], in0=ps[bB][:, 0:T], in1=wk[i_y][:, 0:T],
                                                 op=ALU.mult), reads=[B_ps[bB], B_wk[i_y]], writes=[B_wk[i_y]])
            tr.op(POOL, lambda e: e.tensor_tensor(out=wk[i_sq][:, 0:T], in0=wk[i_y][:, 0:T], in1=wk[i_y][:, 0:T],
                                                  op=ALU.mult), reads=[B_wk[i_y]], writes=[B_wk[i_sq]])

            def ctail():
                tr.op(PE, lambda e: e.matmul(ps[bU][:, 0:T], lhsT=onesm[:], rhs=wk[i_sq][:, 0:T], start=True, stop=True),
                      reads=[B_wk[i_sq]], writes=[B_ps[bU]])
                tr.op(ACT, lambda e: e.activation(out=wk[i_sq][:, 0:T], in_=ps[bU][:, 0:T], func=AF.Sqrt,
                                                  bias=eps_t[:, 0:1], scale=1.0), reads=[B_ps[bU]], writes=[B_wk[i_sq]])
                tr.op(DVE, lambda e: e.reciprocal(out=wk[i_sq][:, 0:T], in_=wk[i_sq][:, 0:T]),
                      reads=[B_wk[i_sq]], writes=[B_wk[i_sq]])
                tr.op(POOL, lambda e: e.tensor_scalar(out=wk[i_y][:, 0:T], in0=wk[i_y][:, 0:T], scalar1=gc[:, g:g + 1],
                                                      scalar2=None, op0=ALU.mult),
                      reads=[B_wk[i_y]], writes=[B_wk[i_y]])
                tr.op(POOL, lambda e: e.tensor_tensor(out=out_ap, in0=wk[i_y][:, 0:T], in1=wk[i_sq][:, 0:T],
                                                      op=ALU.mult),
                      reads=[B_wk[i_y], B_wk[i_sq]], writes=[out_buf])
                post()
            defer(ctail)

        def pre_tile(slot, t, own_idx, pumps):
            halo = xh_d[own_idx] if own_idx is not None else None
            rms_prologue(xs[slot][t * T:(t + 1) * T, :], halo, slot * c.NT + t)
            for kvh in range(NKV):
                si = wload(ws_in[s_in_index(c.ok + kvh * 128)], KC * 128)
                o_t, o_b, o_s = stage_out()

                def post(kvh=kvh, o_t=o_t, o_b=o_b, o_s=o_s):
                    tok = tr.dma(POOL, o_s, kS[slot * NKV + kvh, :, t * T:(t + 1) * T], o_t[:], reads=[o_b], writes=[])
                    note_store(tok, kv_toks)
                qk_head(si, ((0, 1, 2) if kvh % 2 == 0 else (3, 4, 5)), gk[:, 0:1], o_t[:], o_b, post, tail_eng=DVE)
                hook()
            flush()
            VW = c.KVW
            for kt in range(NVT):
                si = wload(ws_v[kt], c.MK * VW)

                def f(e, si=si, kt=kt):
                    ins = None
                    for tb in range(TB):
                        for kc in range(c.MK):
                            ins = e.matmul(ps[tb][:, 0:VW], lhsT=hT[:, kt * c.MK + kc, tb * 128:(tb + 1) * 128],
                                           rhs=wb[si][:, kc * VW:(kc + 1) * VW],
                                           start=(kt == 0 and kc == 0), stop=(kt == NVT - 1 and kc == c.MK - 1))
                    return ins
                tr.op(PE, f, reads=[B_wb[si], B_hT], writes=[B_ps[tb] for tb in range(TB)])
                hook()
            for tb in range(TB):
                o_t, o_b, o_s = stage_out()
                tr.op(ACT, lambda e: e.activation(out=o_t[:, 0:VW], in_=ps[tb][:, 0:VW], func=AF.Copy),
                      reads=[B_ps[tb]], writes=[o_b])
                ktile = t * TB + tb
                dst = vS[slot * NKV:(slot + 1) * NKV, :, ktile * 128:(ktile + 1) * 128].rearrange("h p d -> p h d")
                srcv = o_t[:, 0:VW].rearrange("p (h d) -> p h d", h=NKV)
                tok = tr.dma(POOL, o_s, dst, srcv, reads=[o_b], writes=[])
                note_store(tok, kv_toks)
            if own_idx is None:
                return
            for h in range(NH):
                si = wload(ws_in[s_in_index(c.oq + h * 128)], KC * 128)
                o_t, o_b, o_s = stage_out()

                def postq(h=h, o_t=o_t, o_b=o_b, o_s=o_s):
                    tok = tr.dma(POOL, o_s, qS[own_idx, :, h * T:(h + 1) * T], o_t[:], reads=[o_b], writes=[])
                    note_store(tok, kv_toks)
                qk_head(si, ((0, 1, 2) if h % 2 == 0 else (3, 4, 5)), gq[:, 0:1], o_t[:], o_b, postq)
                hook()
            for g in range(CG):
                o_t, o_b, o_s = stage_out()

                def postc(g=g, o_t=o_t, o_b=o_b, o_s=o_s):
                    tok = tr.dma(POOL, o_s, cS[own_idx, :, g * T:(g + 1) * T], o_t[:], reads=[o_b], writes=[])
                    note_store(tok, kv_toks)
                conv_group(g, o_t[:], o_b, postc)
                hook()
            flush()

        def main_tile(slot, t, own_idx):
            x_src = xs[slot][t * T:(t + 1) * T, :]
            for tb in range(TB):
                tr.dma(SP, d_x[tb], xres[:, tb, :], x_src[tb * 128:(tb + 1) * 128, :], writes=[B_xr[tb]])
            transfer([B_gfin], [B_qT])
            tr.dma(SP, d_qs, qT[:, :], qS[own_idx], writes=[B_qT])
            transfer(B_act, [B_acT])
            tr.dma(SP, d_cs, acT_flat[:, NH * T:(NH + CG) * T], cS[own_idx], writes=[B_acT])
            transfer([B_hT], B_kv)
            scale = 128.0 ** -0.5
            for h in range(NH):
                kvh = h // c.G
                kb = kvh % 2
                if h % c.G == 0:
                    tr.dma(SP, d_kv[kb], kT_view(kb), kS[slot * NKV + kvh], reads=[B_kS[slot]], writes=[B_kv[kb]])
                    tr.dma(SP, d_kv[kb], v_view(kb), vS[slot * NKV + kvh], reads=[B_vS[slot]], writes=[B_kv[kb]])
                bo = 4 + (h % 2)
                bsum = 6 + (h % 2)
                KT = c.KT
                kTv, vv = kT_view(kb), v_view(kb)

                def s_mm(kt):
                    b = kt % 4
                    tr.op(PE, lambda e: e.matmul(ps[b][:, 0:T], lhsT=kTv[:, kt * 128:(kt + 1) * 128],
                                                 rhs=qT[:, h * T:(h + 1) * T], start=True, stop=True),
                          reads=[B_kv[kb], B_qT], writes=[B_ps[b]])
                s_mm(0)
                if KT > 1:
                    s_mm(1)
                pbufs = [(pTb[i], B_pT[i]) for i in range(NPT)] + [(hp[i], B_hp[i]) for i in range(2)]
                assert KT % 4 == 0
                grp = []
                for kt in range(KT):
                    b = kt % 4
                    pb_t, pb_b = pbufs[(h * KT + kt) % len(pbufs)]
                    tr.op(ACT, lambda e: e.activation(out=pb_t[:, 0:T], in_=ps[b][:, 0:T], func=AF.Exp, scale=scale,
                                                      bias=negc[:, 0:1]),
                          reads=[B_ps[b]], writes=[pb_b])
                    if kt + 2 < KT:
                        s_mm(kt + 2)
                    if kt == min(10, KT - 1):
                        flush()
                    tr.op(PE, lambda e: e.matmul(ps[bo][:, 0:T], lhsT=vv[:, kt * 128:(kt + 1) * 128], rhs=pb_t[:, 0:T],
                                                 start=(kt == 0), stop=(kt == KT - 1)),
                          reads=[B_kv[kb], pb_b], writes=[B_ps[bo]])
                    grp.append((pb_t, pb_b))
                    if len(grp) == 4:
                        def sum4(e, grp=tuple(grp), g0=(kt // 4 == 0), g1=(kt == KT - 1)):
                            ins = None
                            for k, (pt, _) in enumerate(grp):
                                ins = e.matmul(ps[bsum][32 * k:32 * k + 32, 0:T], lhsT=ones_b[:, 0:32], rhs=pt[:, 0:T],
                                               start=g0, stop=g1, tile_position=(0, 32 * k))
                            return ins
                        tr.op(PE, sum4, reads=[pbb for _, pbb in grp], writes=[B_ps[bsum]])
                        grp = []
                i_rs, i_a, i_sq = getwk(), getwk(), getwk()
                tr.op(DVE, lambda e: e.tensor_copy(out=wk[i_rs][:, 0:T], in_=ps[bsum][:, 0:T]),
                      reads=[B_ps[bsum]], writes=[B_wk[i_rs]])
                tr.op(PE, lambda e: e.matmul(ps[bsum][:, 0:T], lhsT=ones32[:], rhs=wk[i_rs][:, 0:T], start=True, stop=True),
                      reads=[B_wk[i_rs]], writes=[B_ps[bsum]])
                tr.op(DVE, lambda e: e.reciprocal(out=wk[i_rs][:, 0:T], in_=ps[bsum][:, 0:T]),
                      reads=[B_ps[bsum]], writes=[B_wk[i_rs]])
                tr.op(DVE, lambda e: e.tensor_tensor(out=wk[i_a][:, 0:T], in0=ps[bo][:, 0:T], in1=wk[i_rs][:, 0:T],
                                                     op=ALU.mult), reads=[B_ps[bo], B_wk[i_rs]], writes=[B_wk[i_a]])
                tr.op(POOL, lambda e: e.tensor_tensor(out=wk[i_sq][:, 0:T], in0=wk[i_a][:, 0:T], in1=wk[i_a][:, 0:T],
                                                      op=ALU.mult), reads=[B_wk[i_a]], writes=[B_wk[i_sq]])
                def atail(h=h, bsum=bsum, i_a=i_a, i_sq=i_sq):
                    tr.op(PE, lambda e: e.matmul(ps[bsum][:, 0:T], lhsT=onesm[:], rhs=wk[i_sq][:, 0:T], start=True, stop=True),
                          reads=[B_wk[i_sq]], writes=[B_ps[bsum]])
                    tr.op(ACT, lambda e: e.activation(out=wk[i_sq][:, 0:T], in_=ps[bsum][:, 0:T], func=AF.Ln,
                                                      bias=eps_t[:, 0:1], scale=1.0), reads=[B_ps[bsum]], writes=[B_wk[i_sq]])
                    tr.op(ACT, lambda e: e.activation(out=wk[i_sq][:, 0:T], in_=wk[i_sq][:, 0:T], func=AF.Exp, scale=-0.5),
                          reads=[B_wk[i_sq]], writes=[B_wk[i_sq]])
                    tr.op(POOL, lambda e: e.tensor_scalar(out=wk[i_a][:, 0:T], in0=wk[i_a][:, 0:T], scalar1=ga[:, h:h + 1],
                                                          scalar2=None, op0=ALU.mult),
                          reads=[B_wk[i_a]], writes=[B_wk[i_a]])
                    tr.op(POOL, lambda e: e.tensor_tensor(out=acT[:, h, :], in0=wk[i_a][:, 0:T], in1=wk[i_sq][:, 0:T],
                                                          op=ALU.mult),
                          reads=[B_wk[i_a], B_wk[i_sq]], writes=[B_acT])
                pend["f"] = atail
            flush()
            transfer([B_qT], [B_gfin])
            tr.dma(SP, d_gf, gfin[:, 0:D], gfin_d, writes=[B_gfin])
            CW = c.CW
            inc_stats = (CW == 512)
            if inc_stats:
                stats_reset()
            for n in range(c.NCG):
                banks = [(n % 2) * 4 + tb for tb in range(TB)]
                for kt in range(NOT):
                    si = wload(ws_out[n * NOT + kt], c.MK * CW)

                    def f(e, si=si, kt=kt, banks=banks):
                        ins = None
                        for tb in range(TB):
                            for kc in range(c.MK):
                                ins = e.matmul(ps[banks[tb]][:, 0:CW], lhsT=acT[:, kt * c.MK + kc, tb * 128:(tb + 1) * 128],
                                               rhs=wb[si][:, kc * CW:(kc + 1) * CW],
                                               start=(kt == 0 and kc == 0), stop=(kt == NOT - 1 and kc == c.MK - 1))
                        return ins
                    tr.op(PE, f, reads=[B_wb[si], B_acT], writes=[B_ps[b] for b in banks])
                for tb in range(TB):
                    xs_ap = xres[:, tb, n * CW:(n + 1) * CW]
                    tr.op(DVE, lambda e: e.tensor_tensor(out=xs_ap, in0=ps[banks[tb]][:, 0:CW], in1=xs_ap, op=ALU.add),
                          reads=[B_ps[banks[tb]], B_xr[tb]], writes=[B_xr[tb]])
                    if inc_stats:
                        sq_piece(tb, n)
            transfer(B_kv, [B_hT])
            rms_prologue(None, None, None, stats_done=inc_stats)
            transfer([B_acT], B_act)

            def up(j):
                a = j % 2
                for cbi in range(HCK):
                    si = wload(ws_up[j * HCK + cbi], KC * 128)
                    b = cbi % 2
                    proj_S(si, b)
                    i_r = getwk()
                    tr.op(ACT, lambda e: e.activation(out=wk[i_r][:, 0:T], in_=ps[b][:, 0:T], func=AF.Relu),
                          reads=[B_ps[b]], writes=[B_wk[i_r]])
                    tr.op(POOL, lambda e: e.tensor_tensor(out=act_view(a)[:, cbi * T:(cbi + 1) * T], in0=wk[i_r][:, 0:T],
                                                          in1=wk[i_r][:, 0:T], op=ALU.mult),
                          reads=[B_wk[i_r]], writes=[B_act[a]])
            dn_rr = {"n": 0}

            def down(j):
                a = j % 2
                for n in range(c.NCG):
                    si = wload(ws_dn[j * c.NCG + n], HCK * CW)
                    for tb in range(TB):
                        b = 2 + dn_rr["n"] % 6
                        dn_rr["n"] += 1

                        def f(e, si=si, tb=tb, b=b):
                            ins = None
                            for kc in range(HCK):
                                ins = e.matmul(ps[b][:, 0:CW], lhsT=act_view(a)[:, kc * T + tb * 128: kc * T + (tb + 1) * 128],
                                               rhs=wb[si][:, kc * CW:(kc + 1) * CW], start=(kc == 0), stop=(kc == HCK - 1))
                            return ins
                        tr.op(PE, f, reads=[B_wb[si], B_act[a]], writes=[B_ps[b]])
                        xs_ap = xres[:, tb, n * CW:(n + 1) * CW]
                        tr.op(DVE, lambda e: e.tensor_tensor(out=xs_ap, in0=ps[b][:, 0:CW], in1=xs_ap, op=ALU.add),
                              reads=[B_ps[b], B_xr[tb]], writes=[B_xr[tb]])
                        if inc_stats and j == c.NCH - 1:
                            sq_piece(tb, n)
            up(0)
            for j in range(c.NCH):
                if j + 1 < c.NCH:
                    up(j + 1)
                if inc_stats and j == c.NCH - 1:
                    stats_reset()
                down(j)
            if not inc_stats:
                stats_reset()
                for tb in range(TB):
                    for pc in range(NPC):
                        sq_piece(tb, pc)
            stats_finish()
            FW = min(1024, D)
            tr.wait_tok(DVE, B_rstd.w)
            for tb in range(TB):
                for pc in range(D // FW):
                    xs_ap = xres[:, tb, pc * FW:(pc + 1) * FW]
                    tr.op(DVE, lambda e: e.scalar_tensor_tensor(out=xs_ap, in0=xs_ap, scalar=rstd[:, tb:tb + 1],
                                                                in1=gfin[:, pc * FW:(pc + 1) * FW],
                                                                op0=ALU.mult, op1=ALU.mult),
                          reads=[B_xr[tb], B_rstd, B_gfin], writes=[B_xr[tb]], same_ok=True)
                r0 = own_idx * T + tb * 128
                tok = tr.dma(SP, d_y[tb], y_d[r0:r0 + 128, :], xres[:, tb, :], reads=[B_xr[tb]], writes=[])
                note_store(tok, y_toks)

        transfer(B_kv, [B_hT])
        nonown = [(sl, t) for sl in range(2) if nown[sl] > 0 for t in range(nown[sl], c.NT)]
        owns = [(sl, t) for sl in range(2) for t in range(nown[sl])]
        n_win = nsteps(c.AW, D, KG) + nsteps(3 * c.CWD, D, KG)
        n_b = n_win
        n_pieces = TB * (-(-KC // 4))
        hooks_b = n_pieces + NKV + NVT
        hk["rate"] = n_b / max(1, len(nonown) * hooks_b) * 1.05
        for sl, t in nonown:
            pre_tile(sl, t, None, None)
        pump(max(0, n_b - pumped["n"]))
        for sem, val in store_toks.items():
            tr.wait_tok(SP, (sem, val))
        n_c = max(0, n_rest - pumped["n"])
        hooks_c = hooks_b + NH + 3 * CG
        hk["rate"] = n_c / max(1, len(owns) * hooks_c) * 1.03
        hk["acc"] = 0.0
        for own, (sl, t) in enumerate(owns):
            pre_tile(sl, t, own, None)
        hk["rate"] = 0.0
        for sem, val in list(kv_toks.items()):
            tr.wait_tok(SP, (sem, val))
        p0_barrier()
        for own, (sl, t) in enumerate(owns):
            main_tile(sl, t, own)
        for sem, val in y_toks.items():
            tr.wait_tok(SP, (sem, val))
    return nc


def rope_tables(pos):
    row = (pos // GRID_W).astype(np.float32)
    col = (pos % GRID_W).astype(np.float32)
    freqs = (np.float32(ROPE_THETA) ** (-np.arange(0, 64, 2, dtype=np.float32) / np.float32(64))).astype(np.float32)
    d = np.arange(128)
    f = freqs[d % 32]
    p = np.where((d // 64)[:, None] == 0, row[None, :], col[None, :]).astype(np.float32)
    ang = (p * f[:, None]).astype(np.float32)
    return np.cos(ang).astype(np.float32), np.sin(ang).astype(np.float32)


def make_consts():
    ident = np.eye(128, dtype=np.float32)
    rm = np.zeros((128, 128), np.float32)
    for m in range(128):
        if (m % 64) < 32:
            rm[m + 32, m] = -1.0
        else:
            rm[m - 32, m] = 1.0
    return ident, rm


def plan_cores(cfg, n_seq, n_cores):
    c = cfg
    plans = []
    for core in range(n_cores):
        if c.nB > 0:
            nAseq = n_cores * c.nA // c.NT
            sa = core * c.nA // c.NT
            ta = [(core * c.nA) % c.NT + i for i in range(c.nA)]
            sbq = nAseq + core * c.nB // c.NT
            tbl = [(core * c.nB) % c.NT + i for i in range(c.nB)]
            plans.append([(sa, ta), (sbq, tbl)])
        else:
            sa = core * c.nA // c.NT
            ta = [(core * c.nA) % c.NT + i for i in range(c.nA)]
            plans.append([(sa, ta), (sa, [])])
    return plans


def host_inputs(cfg, xseqs, w, plans):
    c = cfg
    D, S = c.D, c.S
    ident, rm = make_consts()
    common = {
        "w_in": np.ascontiguousarray(w["w_in"]), "w_out": np.ascontiguousarray(w["w_out"]),
        "w_up": np.ascontiguousarray(w["w_up"]), "w_down": np.ascontiguousarray(w["w_down"]),
        "gmix": np.ascontiguousarray(w["mix_norm"].reshape(c.KC, 128).T),
        "gmlp": np.ascontiguousarray(w["mlp_norm"].reshape(c.KC, 128).T),
        "gq": np.ascontiguousarray(w["q_norm"].reshape(128, 1)),
        "gk": np.ascontiguousarray(w["k_norm"].reshape(128, 1)),
        "convw": np.ascontiguousarray(w["conv_w"].reshape(3, c.CG, 128).transpose(2, 1, 0).reshape(128, c.CG * 3)),
        "ga": np.ascontiguousarray(w["attn_grp_norm"].reshape(c.NH, 128).T),
        "gc": np.ascontiguousarray(w["conv_grp_norm"].reshape(c.CG, 128).T),
        "gfin": np.ascontiguousarray(np.broadcast_to(w["final_norm"].reshape(1, D), (128, D))),
        "ident": ident, "rm": rm,
    }
    in_maps = []
    for plan in plans:
        m = dict(common)
        halos = []
        ct = np.zeros((2 * c.NT, 128, T), np.float32)
        st = np.zeros((2 * c.NT, 128, T), np.float32)
        for slot, (seq, tiles) in enumerate(plan):
            order = list(tiles) + [t for t in range(c.NT) if t not in tiles]
            pos = np.concatenate([np.arange(t * T, (t + 1) * T) for t in order])
            xs = xseqs[seq]
            m["xA" if slot == 0 else "xB"] = np.ascontiguousarray(xs[pos])
            cs, sn = rope_tables(pos)
            ct[slot * c.NT:(slot + 1) * c.NT] = cs.reshape(128, c.NT, T).transpose(1, 0, 2)
            st[slot * c.NT:(slot + 1) * c.NT] = sn.reshape(128, c.NT, T).transpose(1, 0, 2)
            for t in tiles:
                prev = xs[t * T - 1] if t * T - 1 >= 0 else np.zeros(D, np.float32)
                nxt = xs[(t + 1) * T] if (t + 1) * T < S else np.zeros(D, np.float32)
                halos.append(prev)
                halos.append(nxt)
        hh = np.stack(halos).astype(np.float32).reshape(len(halos) // 2, 2, c.KC, 128)
        m["xh"] = np.ascontiguousarray(hh.transpose(0, 3, 1, 2).reshape(len(halos) // 2, 128, 2 * c.KC))
        m["ctab"], m["stab"] = ct, st
        in_maps.append(m)
    return in_maps


def run(cfg, xseqs, w, n_cores=8):
    plans = plan_cores(cfg, xseqs.shape[0], n_cores)
    in_maps = host_inputs(cfg, xseqs, w, plans)
    nc = build(cfg)
    res = run_bass_kernel_spmd(nc, in_maps, core_ids=list(range(n_cores)))
    out = np.zeros_like(xseqs)
    for core, plan in enumerate(plans):
        y = res.results[core]["y"]
        o = 0
        for seq, tiles in plan:
            for t in tiles:
                out[seq, t * T:(t + 1) * T] = y[o * T:(o + 1) * T]
                o += 1
    return out


def kernel(x_prompt, x_sample, w_in, q_norm, k_norm, conv_w, attn_grp_norm, conv_grp_norm, w_out, mix_norm,
           mlp_norm, w_up, w_down, final_norm):
    f = lambda a: np.asarray(a, dtype=np.float32)
    xseqs = np.concatenate([f(x_prompt), f(x_sample)], axis=0)
    w = {"w_in": f(w_in)[0], "w_out": f(w_out)[0], "w_up": f(w_up)[0], "w_down": f(w_down)[0],
         "mix_norm": f(mix_norm)[0], "mlp_norm": f(mlp_norm)[0], "q_norm": f(q_norm)[0], "k_norm": f(k_norm)[0],
         "conv_w": f(conv_w)[0], "attn_grp_norm": f(attn_grp_norm)[0], "conv_grp_norm": f(conv_grp_norm)[0],
         "final_norm": f(final_norm)}
    out = run(FULL, xseqs, w, 8)
    nb = x_prompt.shape[0]
    return (np.ascontiguousarray(out[:nb]), np.ascontiguousarray(out[nb:]))
```

```python
from contextlib import ExitStack
import numpy as np
import concourse.bass as bass
import concourse.mybir as mybir
from concourse.bass_utils import run_bass_kernel_spmd

F32 = mybir.dt.float32
BF16 = mybir.dt.bfloat16
AF = mybir.ActivationFunctionType
ALU = mybir.AluOpType
AX = mybir.AxisListType

EPS = 1e-6
GRID_W = 64
ROPE_THETA = 10000.0
T = 512
TB = 4
HCK = 8
NWB = 4


class Cfg:
    def __init__(self, D, NH, NKV, CG, DFF, S, nA, nB):
        self.D, self.NH, self.NKV, self.CG, self.DFF, self.S, self.nA, self.nB = D, NH, NKV, CG, DFF, S, nA, nB
        self.KC = D // 128
        self.AW = NH * 128
        self.KVW = NKV * 128
        self.CWD = CG * 128
        assert self.AW + self.CWD == D
        self.IN = self.AW + 2 * self.KVW + 3 * self.CWD
        self.FC = DFF // 128
        self.NCH = self.FC // HCK
        self.MK = min(8, self.KC)
        self.CW = min(512, D)
        self.NCG = D // self.CW
        self.NT = S // T
        self.KT = S // 128
        self.WE = max(self.KC * 128, self.MK * max(self.CW, self.KVW), HCK * self.CW)
        self.G = NH // NKV
        self.oq, self.ok, self.ov = 0, self.AW, self.AW + self.KVW
        self.oB = self.AW + 2 * self.KVW
        self.oC = self.oB + self.CWD
        self.ou = self.oC + self.CWD


FULL = Cfg(D=4096, NH=16, NKV=4, CG=16, DFF=16384, S=4096, nA=4, nB=2)


class Sem:
    def __init__(self, nc, es, name):
        self.h = es.enter_context(nc.semaphore(name))
        self.n = 0
        self.name = name


class Engine:
    def __init__(self, name, eng, sem):
        self.name, self.eng, self.sem = name, eng, sem
        self.waited = {}


class Buf:
    def __init__(self, name):
        self.name = name
        self.w = None
        self.r = {}


class Tracker:
    def __init__(self):
        self.engines = []

    def _waits(self, E, reads, writes, is_dma, same_ok=False):
        raw, war = {}, {}

        def add(d, tok):
            if tok is None:
                return
            sem, val = tok
            if d.get(sem, 0) < val:
                d[sem] = val
        for b in reads:
            add(raw, b.w)
        for b in writes:
            add(raw, b.w)
            for sem, val in b.r.items():
                add(war, (sem, val))
        need = dict(raw)
        for sem, val in war.items():
            if sem is E.sem and not is_dma:
                continue
            if need.get(sem, 0) < val:
                need[sem] = val
        for sem, val in need.items():
            if sem is E.sem and not is_dma and (same_ok or E.name in ("pe", "sp")):
                continue
            if E.waited.get(sem, 0) >= val:
                continue
            E.eng.wait_ge(sem.h, val)
            E.waited[sem] = val

    def _commit(self, tok, reads, writes):
        for b in writes:
            b.w = tok
            b.r = {}
        for b in reads:
            sem, val = tok
            if b.r.get(sem, 0) < val:
                b.r[sem] = val

    def op(self, E, fn, reads=(), writes=(), same_ok=False):
        self._waits(E, reads, writes, False, same_ok)
        ins = fn(E.eng)
        E.sem.n += 1
        ins.then_inc(E.sem.h, 1)
        tok = (E.sem, E.sem.n)
        self._commit(tok, reads, writes)
        return tok

    def dma(self, E, dsem, out_ap, in_ap, reads=(), writes=()):
        self._waits(E, reads, writes, True)
        ins = E.eng.dma_start(out=out_ap, in_=in_ap)
        dsem.n += 1
        ins.then_inc(dsem.h, 16)
        tok = (dsem, 16 * dsem.n)
        self._commit(tok, reads, writes)
        return tok

    def wait_tok(self, E, tok):
        if tok is None:
            return
        sem, val = tok
        if E.waited.get(sem, 0) >= val:
            return
        E.eng.wait_ge(sem.h, val)
        E.waited[sem] = val


def transfer(srcs, dsts):
    for d in dsts:
        for s in srcs:
            for tok in ([s.w] if s.w is not None else []) + list(s.r.items()):
                sem, val = tok
                if d.r.get(sem, 0) < val:
                    d.r[sem] = val


def build(cfg, debug_stage=None):
    c = cfg
    D, KC, NH, NKV, CG, S = c.D, c.KC, c.NH, c.NKV, c.CG, c.S
    nc = bass.Bass("TRN2", target_bir_lowering=False)

    def din(name, shape, dt=F32):
        return nc.dram_tensor(name, list(shape), dt, kind="ExternalInput").ap()

    def dscr(name, shape, dt=BF16):
        return nc.dram_tensor(name, list(shape), dt, kind="Internal").ap()

    xs = [din("xA", [S, D]), din("xB", [S, D])]
    nown = [c.nA, c.nB]
    NOWN = c.nA + c.nB
    xh_d = din("xh", [NOWN, 128, 2 * KC])
    ctab_d = din("ctab", [2 * c.NT, 128, T])
    stab_d = din("stab", [2 * c.NT, 128, T])
    w_in_d = din("w_in", [D, c.IN])
    w_out_d = din("w_out", [D, D])
    w_up_d = din("w_up", [D, c.DFF])
    w_down_d = din("w_down", [c.DFF, D])
    gmix_d = din("gmix", [128, KC])
    gmlp_d = din("gmlp", [128, KC])
    gq_d = din("gq", [128, 1])
    gk_d = din("gk", [128, 1])
    convw_d = din("convw", [128, CG * 3])
    ga_d = din("ga", [128, NH])
    gc_d = din("gc", [128, CG])
    gfin_d = din("gfin", [128, D])
    ident_d = din("ident", [128, 128])
    rm_d = din("rm", [128, 128])
    y_d = nc.dram_tensor("y", [NOWN * T, D], F32, kind="ExternalOutput").ap()

    n_s_in = (c.AW + c.KVW + 3 * c.CWD) // 128
    ws_in = dscr("ws_in", [n_s_in, 128, KC * 128])
    NVT = KC // c.MK
    ws_v = dscr("ws_v", [NVT, 128, c.MK * c.KVW])
    NOT = KC // c.MK
    ws_out = dscr("ws_out", [c.NCG * NOT, 128, c.MK * c.CW])
    ws_up = dscr("ws_up", [c.FC, 128, KC * 128])
    ws_dn = dscr("ws_dn", [c.NCH * c.NCG, 128, HCK * c.CW])
    kS = dscr("kS", [2 * NKV, 128, S])
    vS = dscr("vS", [2 * NKV, 128, c.KT * 128])
    qS = dscr("qS", [NOWN, 128, NH * T])
    cS = dscr("cS", [NOWN, 128, CG * T])

    def s_in_index(col):
        cb = col // 128
        if col >= c.ov + c.KVW:
            cb -= c.KVW // 128
        return cb

    tr = Tracker()
    with ExitStack() as es:
        E_ = es.enter_context

        def sb(name, shape, dt):
            return E_(nc.sbuf_tensor(name, list(shape), dt))

        xres = sb("xres", [128, TB, D], F32)
        hT = sb("hT", [128, KC, T], BF16)
        acT = sb("acT", [128, KC, T], BF16)
        qT = sb("qT", [128, NH * T], BF16)
        assert NH * T * 2 >= D * 4
        wb = [sb(f"wb{i}", [128, c.WE], BF16) for i in range(NWB)]
        NWK = 8
        wk = [sb(f"wk{i}", [128, T + 2], F32) for i in range(NWK)]
        NPT = 4
        pTb = [sb(f"pT{i}", [128, T], BF16) for i in range(NPT)]
        hp = [sb(f"hp{i}", [128, 512], BF16) for i in range(2)]
        hTh = sb("hTh", [128, KC * 2], BF16)
        xhs = sb("xhs", [128, 2 * KC], F32)
        xhq = sb("xhq", [128, 2 * KC], F32)
        ctab = sb("ctab_sb", [128, T], F32)
        stab = sb("stab_sb", [128, T], F32)
        stat = sb("stat", [128, 64], F32)
        st2 = sb("st2", [128, 8], F32)
        rstd = sb("rstd", [128, 8], F32)
        uh = sb("uh", [128, 4], F32)
        ident_f = sb("ident_f", [128, 128], F32)
        ident = sb("ident_b", [128, 128], BF16)
        rm = sb("rm_sb", [128, 128], F32)
        onesm = sb("onesm", [128, 128], F32)
        ones_b = sb("ones_b", [128, 128], BF16)
        eps_t = sb("eps_t", [128, 1], F32)
        gmix = sb("gmix_sb", [128, KC], F32)
        gmlp = sb("gmlp_sb", [128, KC], F32)
        gq = sb("gq_sb", [128, 1], F32)
        gk = sb("gk_sb", [128, 1], F32)
        convw = sb("convw_sb", [128, CG * 3], F32)
        ga = sb("ga_sb", [128, NH], F32)
        gc = sb("gc_sb", [128, CG], F32)

        hT_flat = hT[:].rearrange("p k t -> p (k t)")
        kvsz = S + c.KT * 128
        assert 2 * kvsz <= KC * T, "kv double buffer must fit in hT"
        acT_flat = acT[:].rearrange("p k t -> p (k t)")
        assert 2 * HCK * T <= KC * T
        gfin = qT[:].bitcast(F32)

        def kT_view(i):
            return hT_flat[:, i * kvsz: i * kvsz + S]

        def v_view(i):
            return hT_flat[:, i * kvsz + S: (i + 1) * kvsz]

        def act_view(i):
            return acT_flat[:, i * HCK * T: (i + 1) * HCK * T]

        ps = [E_(nc.psum_tensor(f"ps{i}", [128, 512], F32)) for i in range(8)]
        ps7b = ps[7][:].bitcast(BF16)
        ps6b = ps[6][:].bitcast(BF16)
        ptb = [ps6b, ps7b]

        def mk(name):
            return Sem(nc, es, name)
        PE = Engine("pe", nc.tensor, mk("s_pe"))
        ACT = Engine("act", nc.scalar, mk("s_act"))
        DVE = Engine("dve", nc.vector, mk("s_dve"))
        POOL = Engine("pool", nc.gpsimd, mk("s_pool"))
        SP = Engine("sp", nc.sync, mk("s_sp"))
        d_w = [mk(f"d_w{i}") for i in range(NWB)]
        d_x = [mk(f"d_x{i}") for i in range(TB)]
        d_xh, d_ct, d_st = mk("d_xh"), mk("d_ct"), mk("d_st")
        d_kv = [mk("d_kv0"), mk("d_kv1")]
        d_gf = mk("d_gf")
        d_qs, d_cs = mk("d_qs"), mk("d_cs")
        d_y = [mk(f"d_y{i}") for i in range(TB)]
        d_ks = [mk("d_ks0"), mk("d_ks1")]
        d_vs = [mk(f"d_vs{i}") for i in range(4)]
        d_c = [mk(f"d_c{i}") for i in range(12)]
        d_p0l = [mk(f"d_p0l{i}") for i in range(4)]
        d_p0s = [mk(f"d_p0s{i}") for i in range(4)]

        B_xr = [Buf(f"xres{i}") for i in range(TB)]
        B_hT, B_acT, B_qT = Buf("hT"), Buf("acT"), Buf("qT")
        B_kv = [Buf("kv0"), Buf("kv1")]
        B_act = [Buf("act0"), Buf("act1")]
        B_gfin = Buf("gfin")
        B_wb = [Buf(f"wb{i}") for i in range(NWB)]
        B_wk = [Buf(f"wk{i}") for i in range(NWK)]
        B_pT = [Buf(f"pT{i}") for i in range(NPT)]
        B_hp = [Buf("hp0"), Buf("hp1")]
        B_hTh, B_xhq, B_xhal = Buf("hTh"), Buf("xhq"), Buf("xhal")
        B_ctab, B_stab = Buf("ctab"), Buf("stab")
        kst = [pTb[0], pTb[1]]
        B_kst = [B_pT[0], B_pT[1]]
        B_stat, B_st2, B_rstd, B_uh = Buf("stat"), Buf("st2"), Buf("rstd"), Buf("uh")
        B_ps = [Buf(f"ps{i}") for i in range(8)]
        B_p7h = [Buf("p7h0"), Buf("p7h1")]
        B_const = Buf("const")
        B_wsin, B_wsv, B_wsout, B_wsup, B_wsdn = Buf("wsin"), Buf("wsv"), Buf("wsout"), Buf("wsup"), Buf("wsdn")
        B_kS, B_vS = [Buf("kS0"), Buf("kS1")], [Buf("vS0"), Buf("vS1")]
        B_y = Buf("y")
        store_toks = {}
        kv_toks = {}
        y_toks = {}

        def note_store(tok, d=None):
            d = store_toks if d is None else d
            d[tok[0]] = tok[1]

        small = [(gmix, gmix_d), (gmlp, gmlp_d), (gq, gq_d), (gk, gk_d), (convw, convw_d), (ga, ga_d),
                 (gc, gc_d), (ident_f, ident_d), (rm, rm_d)]
        for i, (t_sb, t_d) in enumerate(small):
            tr.dma(SP, d_c[i], t_sb[:], t_d, writes=[B_const])
        tr.op(POOL, lambda e: e.memset(onesm[:], 1.0 / 128.0), writes=[B_const])
        tr.op(POOL, lambda e: e.memset(ones_b[:], 1.0), writes=[B_const])
        tr.op(POOL, lambda e: e.memset(eps_t[:], EPS), writes=[B_const])
        for i in range(len(small)):
            tr.wait_tok(DVE, (d_c[i], 16))
        tr.op(DVE, lambda e: e.tensor_copy(out=ident[:], in_=ident_f[:]), reads=[B_const], writes=[B_const])
        for E in (PE, ACT, POOL):
            for i in range(len(small)):
                tr.wait_tok(E, (d_c[i], 16))
            tr.wait_tok(E, B_const.w)
            tr.wait_tok(E, (POOL.sem, POOL.sem.n))
        tr.wait_tok(DVE, (POOL.sem, POOL.sem.n))
        B_const.w = None
        B_const.r = {}
        gqk = sb("gqk", [128, 2], F32)
        gmx = sb("gmx", [2, 4], F32)
        negc = sb("negc", [128, 1], F32)
        ones_f = sb("ones_f", [128, 128], F32)
        B_sm = Buf("softmax_shift")
        tr.op(ACT, lambda e: e.activation(out=gqk[:, 0:1], in_=gq[:, 0:1], func=AF.Abs), writes=[B_sm])
        tr.op(ACT, lambda e: e.activation(out=gqk[:, 1:2], in_=gk[:, 0:1], func=AF.Abs), reads=[B_sm], writes=[B_sm])
        tr.op(DVE, lambda e: e.memset(ones_f[:], 1.0), reads=[B_sm], writes=[B_sm])
        tr.op(PE, lambda e: e.matmul(ps[0][0:2, 0:128], lhsT=gqk[:, 0:2], rhs=ident_f[:], start=True, stop=True),
              reads=[B_sm], writes=[B_ps[0]])
        tr.op(DVE, lambda e: e.tensor_reduce(out=gmx[0:2, 0:1], in_=ps[0][0:2, 0:128], axis=AX.X, op=ALU.max),
              reads=[B_ps[0]], writes=[B_sm])
        tr.op(DVE, lambda e: e.tensor_scalar(out=gmx[0:2, 2:4], in0=ident_f[0:2, 0:2], scalar1=gmx[0:2, 0:1], scalar2=None,
                                             op0=ALU.mult), reads=[B_sm], writes=[B_sm])
        tr.op(PE, lambda e: e.matmul(ps[1][:, 0:2], lhsT=ones_f[0:2, :], rhs=gmx[0:2, 2:4], start=True, stop=True),
              reads=[B_sm], writes=[B_ps[1]])
        tr.op(DVE, lambda e: e.tensor_copy(out=gqk[:, 0:2], in_=ps[1][:, 0:2]), reads=[B_ps[1]], writes=[B_sm])
        tr.op(DVE, lambda e: e.scalar_tensor_tensor(out=negc[:, 0:1], in0=gqk[:, 0:1], scalar=-(128.0 ** 0.5),
                                                    in1=gqk[:, 1:2], op0=ALU.mult, op1=ALU.mult),
              reads=[B_sm], writes=[B_sm])
        tr.wait_tok(ACT, B_sm.w)

        KG = 4
        NS = 4
        acT_f32 = acT_flat.bitcast(F32)
        if min((KC * T // 2) // (NS * KG), (NH * T) // (NS * KG)) < c.CW:
            KG = 2
        PW = min(512, (KC * T // 2) // (NS * KG), (NH * T) // (NS * KG))

        def stage_f(i):
            return acT_f32[:, i * KG * PW:(i + 1) * KG * PW]

        def stage_b(i):
            return qT[:, i * KG * PW:(i + 1) * KG * PW]
        B_sf = [Buf(f"sf{i}") for i in range(NS)]
        B_sb = [Buf(f"sb{i}") for i in range(NS)]
        p0_step = [0]
        cast_rr = [0]

        def p0(w_d, row0, nk, col0, ncols, uw, kgmax, gain, dst_fn):
            nkc = nk // 128
            kg = min(KG, nkc, kgmax)
            assert PW % uw == 0 and ncols % uw == 0
            for c0 in range(0, ncols, PW):
                pw = min(PW, ncols - c0)
                nu = pw // uw
                for kc0 in range(0, nkc, kg):
                    i = p0_step[0] % NS
                    p0_step[0] += 1
                    r0 = row0 + kc0 * 128
                    sfv = stage_f(i)[:, 0:kg * pw].rearrange("p (k c) -> p k c", k=kg)

                    def load_fn(i=i, sfv=sfv, r0=r0, c0=c0, pw=pw):
                        tr.dma(ACT, d_p0l[i], sfv,
                               w_d[r0:r0 + kg * 128, col0 + c0:col0 + c0 + pw].rearrange("(k p) c -> p k c", p=128),
                               writes=[B_sf[i]])

                    def work_fn(i=i, sfv=sfv, c0=c0, pw=pw, nu=nu, kc0=kc0):
                        sbv = stage_b(i)[:, 0:kg * pw].rearrange("p (u k j) -> p u k j", u=nu, k=kg)
                        for k in range(kg):
                            src = sfv[:, k, :].rearrange("p (u j) -> p u j", u=nu)
                            dst = sbv[:, :, k, :]
                            which = cast_rr[0] % 2
                            cast_rr[0] += 1
                            if gain is None:
                                if which == 0:
                                    tr.op(DVE, lambda e: e.tensor_copy(out=dst, in_=src), reads=[B_sf[i]],
                                          writes=[B_sb[i]], same_ok=True)
                                else:
                                    tr.op(ACT, lambda e: e.activation(out=dst, in_=src, func=AF.Copy),
                                          reads=[B_sf[i]], writes=[B_sb[i]], same_ok=True)
                            else:
                                gk_ = row0 // 128 + kc0 + k
                                g_ap = gain[:, gk_:gk_ + 1]
                                if which == 0:
                                    tr.op(DVE, lambda e: e.tensor_scalar(out=dst, in0=src, scalar1=g_ap, scalar2=None,
                                                                         op0=ALU.mult),
                                          reads=[B_sf[i]], writes=[B_sb[i]], same_ok=True)
                                else:
                                    tr.op(ACT, lambda e: e.activation(out=dst, in_=src, func=AF.Copy, scale=g_ap),
                                          reads=[B_sf[i]], writes=[B_sb[i]], same_ok=True)
                        for E in (DVE, ACT):
                            tr.wait_tok(POOL, (E.sem, E.sem.n))
                        sb3 = stage_b(i)[:, 0:kg * pw].rearrange("p (u x) -> p u x", u=nu)
                        for dram_ap, sb_ap in dst_fn(c0 // uw, nu, kc0, kg, sb3):
                            tok = tr.dma(POOL, d_p0s[i], dram_ap, sb_ap, reads=[B_sb[i]], writes=[])
                            note_store(tok)
                    yield (load_fn, work_fn)

        class StepPipe:
            def __init__(self, gen):
                self.gen = gen
                self.q = []
                self.done = False

            def _fill(self):
                while not self.done and len(self.q) < NS:
                    st = next(self.gen, None)
                    if st is None:
                        self.done = True
                        break
                    st[0]()
                    self.q.append(st[1])

            def step(self):
                self._fill()
                if not self.q:
                    return False
                self.q.pop(0)()
                self._fill()
                return True

        def conv_S(w_d, col0, ncols, gain, ws, tile0):
            def dst_fn(u0, nu, kc0, kg, sb3):
                t0 = tile0 + u0
                return [(ws[t0:t0 + nu, :, kc0 * 128:(kc0 + kg) * 128].rearrange("c p x -> p c x"), sb3)]
            yield from p0(w_d, 0, D, col0, ncols, 128, KG, gain, dst_fn)

        def conv_M(w_d, row0, nk, col0, ncols, cw, mk, gain, ws, tile_fn):
            def dst_fn(u0, nu, kc0, kg, sb3):
                kt, kin = kc0 // mk, kc0 % mk
                assert kin + kg <= mk
                return [(ws[tile_fn(u0 + n, kt), :, kin * cw:(kin + kg) * cw], sb3[:, n, :]) for n in range(nu)]
            assert mk % min(KG, mk) == 0
            yield from p0(w_d, row0, nk, col0, ncols, cw, mk, gain, dst_fn)

        def p0_kv():
            yield from conv_S(w_in_d, c.ok, c.KVW, gmix, ws_in, c.ok // 128)
            yield from conv_M(w_in_d, 0, D, c.ov, c.KVW, c.KVW, c.MK, gmix, ws_v, lambda n, kt: kt)

        def p0_rest():
            yield from conv_S(w_in_d, 0, c.AW, gmix, ws_in, 0)
            yield from conv_S(w_in_d, c.oB, 3 * c.CWD, gmix, ws_in, (c.AW + c.KVW) // 128)
            yield from conv_M(w_out_d, 0, D, 0, D, c.CW, c.MK, None, ws_out, lambda n, kt: n * NOT + kt)
            yield from conv_S(w_up_d, 0, c.DFF, gmlp, ws_up, 0)
            for j in range(c.NCH):
                yield from conv_M(w_down_d, j * HCK * 128, HCK * 128, 0, D, c.CW, HCK, None, ws_dn,
                                  lambda n, kt, j=j: j * c.NCG + n)

        kvpipe = StepPipe(p0_kv())
        while kvpipe.step():
            pass
        for sem, val in store_toks.items():
            tr.wait_tok(SP, (sem, val))
        restpipe = StepPipe(p0_rest())

        def nsteps(ncols, nk, kgmax):
            kg = min(KG, nk // 128, kgmax)
            return -(-ncols // PW) * ((nk // 128) // kg)
        n_rest = (nsteps(c.AW, D, KG) + nsteps(3 * c.CWD, D, KG) + nsteps(D, D, c.MK) + nsteps(c.DFF, D, KG)
                  + c.NCH * nsteps(D, HCK * 128, HCK))
        n_kv_tiles = sum(c.NT for sl in range(2) if nown[sl] > 0)
        per_tile = -(-n_rest // max(1, n_kv_tiles))
        p_head = per_tile // (NKV + 3)
        pump_n = [0, p_head, per_tile - p_head * NKV]

        pumped = {"n": 0}
        hk = {"rate": 0.0, "acc": 0.0}

        def hook():
            hk["acc"] += hk["rate"]
            while hk["acc"] >= 1.0:
                hk["acc"] -= 1.0
                pump(1)

        def pump(n):
            for _ in range(n):
                if not restpipe.step():
                    return
                pumped["n"] += 1

        def p0_barrier():
            pump(1 << 30)
            for E in (SP, PE, ACT, DVE, POOL):
                for sem, val in store_toks.items():
                    tr.wait_tok(E, (sem, val))
                for E2 in (ACT, DVE, POOL):
                    if E2 is not E:
                        tr.wait_tok(E, (E2.sem, E2.sem.n))
            store_toks.clear()


        wring = {"n": 0}

        def wload(src_ap, nelem):
            i = wring["n"] % NWB
            wring["n"] += 1
            tr.dma(SP, d_w[i], wb[i][:, 0:nelem], src_ap, writes=[B_wb[i]])
            return i

        wk_rr = {"n": 0}

        def getwk():
            i = wk_rr["n"] % NWK
            wk_rr["n"] += 1
            return i

        NPC = D // 512 if D >= 512 else 1

        def sq_piece(tb, pc):
            j = getwk()
            tr.op(ACT, lambda e: e.activation(out=wk[j][:, 0:512], in_=xres[:, tb, pc * 512:(pc + 1) * 512],
                                              func=AF.Square, accum_out=stat[:, tb * NPC + pc:tb * NPC + pc + 1]),
                  reads=[B_xr[tb], B_stat], writes=[B_wk[j]])

        def stats_reset():
            tr.op(DVE, lambda e: e.memset(stat[:], 0.0), writes=[B_stat])

        def stats_finish():
            tr.wait_tok(DVE, (ACT.sem, ACT.sem.n))
            tr.op(DVE, lambda e: e.tensor_reduce(out=st2[:, 0:TB], in_=stat[:, 0:TB * NPC].rearrange("p (a b) -> p a b", a=TB),
                                                 axis=AX.X, op=ALU.add), reads=[B_stat], writes=[B_st2])
            tr.op(ACT, lambda e: e.activation(out=st2[:, 0:TB], in_=st2[:, 0:TB], func=AF.Sqrt, bias=eps_t[:, 0:1],
                                              scale=1.0 / D), reads=[B_st2], writes=[B_st2])
            tr.op(DVE, lambda e: e.reciprocal(out=rstd[:, 0:TB], in_=st2[:, 0:TB]), reads=[B_st2], writes=[B_rstd])

        def rms_prologue(x_src, halo_src, tab_idx, stats_done=False):
            if x_src is not None:
                for tb in range(TB):
                    tr.dma(SP, d_x[tb], xres[:, tb, :], x_src[tb * 128:(tb + 1) * 128, :], writes=[B_xr[tb]])
            if tab_idx is not None:
                tr.dma(SP, d_ct, ctab[:], ctab_d[tab_idx], writes=[B_ctab])
                tr.dma(SP, d_st, stab[:], stab_d[tab_idx], writes=[B_stab])
            if halo_src is not None:
                tr.dma(SP, d_xh, xhs[:], halo_src, writes=[B_xhal])
            if not stats_done:
                stats_reset()
                for tb in range(TB):
                    for pc in range(NPC):
                        sq_piece(tb, pc)
            stats_finish()
            it = 0
            for tb in range(TB):
                for g0 in range(0, KC, 4):
                    ng = min(4, KC - g0)
                    i = it % 2
                    it += 1
                    tr.op(DVE, lambda e: e.tensor_scalar(out=hp[i][:, 0:ng * 128], in0=xres[:, tb, g0 * 128:(g0 + ng) * 128],
                                                         scalar1=rstd[:, tb:tb + 1], scalar2=None, op0=ALU.mult),
                          reads=[B_xr[tb], B_rstd], writes=[B_hp[i]])

                    def tp(e, i=i, ng=ng):
                        ins = None
                        for k in range(ng):
                            ins = e.transpose(ptb[i][:, k * 128:(k + 1) * 128],
                                              hp[i][:, k * 128:(k + 1) * 128], ident[:])
                        return ins
                    B_pst = B_ps[6 + i]
                    tr.op(PE, tp, reads=[B_hp[i]], writes=[B_pst])
                    src = ptb[i][:, 0:ng * 128].rearrange("p (k t) -> p k t", k=ng)
                    dst = hT[:, g0:g0 + ng, tb * 128:(tb + 1) * 128]
                    tr.op(ACT, lambda e: e.activation(out=dst, in_=src, func=AF.Copy), reads=[B_pst], writes=[B_hT],
                          same_ok=True)
                    hook()
            if halo_src is not None:
                tr.op(DVE, lambda e: e.tensor_tensor(out=xhq[:], in0=xhs[:], in1=xhs[:], op=ALU.mult),
                      reads=[B_xhal], writes=[B_xhq])
                tr.op(DVE, lambda e: e.tensor_reduce(out=st2[:, 4:6], in_=xhq[:].rearrange("p (r k) -> p r k", r=2),
                                                     axis=AX.X, op=ALU.add), reads=[B_xhq], writes=[B_st2])
                tr.op(PE, lambda e: e.matmul(ps[5][:, 0:2], lhsT=onesm[:], rhs=st2[:, 4:6], start=True, stop=True),
                      reads=[B_st2], writes=[B_ps[5]])
                tr.op(ACT, lambda e: e.activation(out=st2[:, 6:8], in_=ps[5][:, 0:2], func=AF.Sqrt, bias=eps_t[:, 0:1],
                                                  scale=1.0 / KC), reads=[B_ps[5]], writes=[B_st2])
                tr.op(DVE, lambda e: e.reciprocal(out=rstd[:, 4:6], in_=st2[:, 6:8]), reads=[B_st2], writes=[B_rstd])
                hTh_v = hTh[:].rearrange("p (k r) -> p r k", r=2)
                for r in range(2):
                    tr.op(DVE, lambda e, r=r: e.tensor_scalar(out=hTh_v[:, r, :], in0=xhs[:, r * KC:(r + 1) * KC],
                                                              scalar1=rstd[:, 4 + r:5 + r], scalar2=None, op0=ALU.mult),
                          reads=[B_xhal, B_rstd], writes=[B_hTh])

        def proj_S(slot, pbank, halo_bank_ap=None):
            def f(e):
                ins = None
                for kc in range(KC):
                    ins = e.matmul(ps[pbank][:, 0:T], lhsT=wb[slot][:, kc * 128:(kc + 1) * 128], rhs=hT[:, kc, :],
                                   start=(kc == 0), stop=(kc == KC - 1))
                if halo_bank_ap is not None:
                    for kc in range(KC):
                        ins = e.matmul(halo_bank_ap, lhsT=wb[slot][:, kc * 128:(kc + 1) * 128],
                                       rhs=hTh[:, kc * 2:(kc + 1) * 2], start=(kc == 0), stop=(kc == KC - 1))
                return ins
            rd = [B_wb[slot], B_hT] + ([B_hTh] if halo_bank_ap is not None else [])
            wr = [B_ps[pbank]] + ([B_ps[6]] if halo_bank_ap is not None else [])
            tr.op(PE, f, reads=rd, writes=wr)

        pend = {"f": None}

        def defer(fn):
            prev = pend["f"]
            pend["f"] = fn
            if prev is not None:
                prev()

        def flush():
            defer(None)

        def qk_head(slot, bset, gain_ap, out_ap, out_buf, post=None, tail_eng=None):
            bq, bs, br = bset
            proj_S(slot, bq)
            i_sq, i_qg, i_t2 = getwk(), getwk(), getwk()
            tr.op(ACT, lambda e: e.activation(out=wk[i_sq][:, 0:T], in_=ps[bq][:, 0:T], func=AF.Square),
                  reads=[B_ps[bq]], writes=[B_wk[i_sq]])
            tr.op(ACT, lambda e: e.activation(out=wk[i_qg][:, 0:T], in_=ps[bq][:, 0:T], func=AF.Copy, scale=gain_ap),
                  reads=[B_ps[bq]], writes=[B_wk[i_qg]])
            def tail():
                qk_tail(bs, br, i_sq, i_qg, i_t2, out_ap, out_buf, tail_eng or POOL)
                if post is not None:
                    post()
            defer(tail)

        def qk_tail(bs, br, i_sq, i_qg, i_t2, out_ap, out_buf, TE):
            tr.op(PE, lambda e: e.matmul(ps[bs][:, 0:T], lhsT=onesm[:], rhs=wk[i_sq][:, 0:T], start=True, stop=True),
                  reads=[B_wk[i_sq]], writes=[B_ps[bs]])
            tr.op(PE, lambda e: e.matmul(ps[br][:, 0:T], lhsT=rm[:], rhs=wk[i_qg][:, 0:T], start=True, stop=True),
                  reads=[B_wk[i_qg]], writes=[B_ps[br]])
            tr.op(ACT, lambda e: e.activation(out=wk[i_sq][:, 0:T], in_=ps[bs][:, 0:T], func=AF.Sqrt, bias=eps_t[:, 0:1],
                                              scale=1.0), reads=[B_ps[bs]], writes=[B_wk[i_sq]])
            tr.op(DVE, lambda e: e.reciprocal(out=wk[i_sq][:, 0:T], in_=wk[i_sq][:, 0:T]),
                  reads=[B_wk[i_sq]], writes=[B_wk[i_sq]])
            tr.op(DVE, lambda e: e.tensor_tensor(out=wk[i_t2][:, 0:T], in0=ps[br][:, 0:T], in1=stab[:], op=ALU.mult),
                  reads=[B_ps[br], B_stab], writes=[B_wk[i_t2]])
            tr.op(TE, lambda e: e.tensor_tensor(out=wk[i_qg][:, 0:T], in0=wk[i_qg][:, 0:T], in1=ctab[:], op=ALU.mult),
                  reads=[B_wk[i_qg], B_ctab], writes=[B_wk[i_qg]])
            tr.op(TE, lambda e: e.tensor_tensor(out=wk[i_qg][:, 0:T], in0=wk[i_qg][:, 0:T], in1=wk[i_t2][:, 0:T],
                                                  op=ALU.add), reads=[B_wk[i_qg], B_wk[i_t2]], writes=[B_wk[i_qg]])
            tr.op(TE, lambda e: e.tensor_tensor(out=out_ap, in0=wk[i_qg][:, 0:T], in1=wk[i_sq][:, 0:T], op=ALU.mult),
                  reads=[B_wk[i_qg], B_wk[i_sq]], writes=[out_buf])

        st_rr = {"n": 0}

        def stage_out():
            j = st_rr["n"] % NPT
            st_rr["n"] += 1
            return pTb[j], B_pT[j], d_vs[j]

        def conv_group(g, out_ap, out_buf, post):
            bB, bC, bU = (0, 1, 2) if g % 2 == 0 else (3, 4, 5)
            hb = (g % 2) * 8
            sB = wload(ws_in[s_in_index(c.oB + g * 128)], KC * 128)
            proj_S(sB, bB)
            hook()
            sC = wload(ws_in[s_in_index(c.oC + g * 128)], KC * 128)
            proj_S(sC, bC, ps[6][:, hb:hb + 2])
            hook()
            sU = wload(ws_in[s_in_index(c.ou + g * 128)], KC * 128)
            proj_S(sU, bU, ps[6][:, hb + 2:hb + 4])
            i_u, i_cu, i_y, i_sq = getwk(), getwk(), getwk(), getwk()
            tr.op(ACT, lambda e: e.activation(out=wk[i_u][:, 0:T], in_=ps[bU][:, 0:T], func=AF.Copy),
                  reads=[B_ps[bU]], writes=[B_wk[i_u]])
            tr.op(ACT, lambda e: e.activation(out=uh[:, 0:2], in_=ps[6][:, hb + 2:hb + 4], func=AF.Copy),
                  reads=[B_ps[6]], writes=[B_uh])
            tr.op(DVE, lambda e: e.tensor_tensor(out=wk[i_cu][:, 1:T + 1], in0=ps[bC][:, 0:T], in1=wk[i_u][:, 0:T],
                                                 op=ALU.mult), reads=[B_ps[bC], B_wk[i_u]], writes=[B_wk[i_cu]])
            tr.op(DVE, lambda e: e.tensor_tensor(out=wk[i_cu][:, 0:1], in0=ps[6][:, hb:hb + 1], in1=uh[:, 0:1],
                                                 op=ALU.mult), reads=[B_ps[6], B_uh], writes=[B_wk[i_cu]])
            tr.op(DVE, lambda e: e.tensor_tensor(out=wk[i_cu][:, T + 1:T + 2], in0=ps[6][:, hb + 1:hb + 2],
                                                 in1=uh[:, 1:2], op=ALU.mult),
                  reads=[B_ps[6], B_uh], writes=[B_wk[i_cu]])
            tr.op(DVE, lambda e: e.tensor_scalar(out=wk[i_y][:, 0:T], in0=wk[i_cu][:, 0:T],
                                                 scalar1=convw[:, g * 3:g * 3 + 1], scalar2=None, op0=ALU.mult),
                  reads=[B_wk[i_cu]], writes=[B_wk[i_y]])
            for jw in (1, 2):
                tr.op(DVE, lambda e, jw=jw: e.scalar_tensor_tensor(out=wk[i_y][:, 0:T], in0=wk[i_cu][:, jw:jw + T],
                                                                   scalar=convw[:, g * 3 + jw:g * 3 + jw + 1],
                                                                   in1=wk[i_y][:, 0:T], op0=ALU.mult, op1=ALU.add),
                      reads=[B_wk[i_cu], B_wk[i_y]], writes=[B_wk[i_y]])
            tr.op(DVE, lambda e: e.tensor_tensor(out=wk[i_y][:, 0:T], in0=ps[bB][:, 0:T], in1=wk[i_y][:, 0:T],
                                                 op=ALU.mult), reads=[B_ps[bB], B_wk[i_y]], writes=[B_wk[i_y]])
            tr.op(DVE, lambda e: e.tensor_tensor(out=wk[i_sq][:, 0:T], in0=wk[i_y][:, 0:T], in1=wk[i_y][:, 0:T],
                                                  op=ALU.mult), reads=[B_wk[i_y]], writes=[B_wk[i_sq]])

            def ctail():
                tr.op(PE, lambda e: e.matmul(ps[bU][:, 0:T], lhsT=onesm[:], rhs=wk[i_sq][:, 0:T], start=True, stop=True),
                      reads=[B_wk[i_sq]], writes=[B_ps[bU]])
                tr.op(ACT, lambda e: e.activation(out=wk[i_sq][:, 0:T], in_=ps[bU][:, 0:T], func=AF.Sqrt,
                                                  bias=eps_t[:, 0:1], scale=1.0), reads=[B_ps[bU]], writes=[B_wk[i_sq]])
                tr.op(DVE, lambda e: e.reciprocal(out=wk[i_sq][:, 0:T], in_=wk[i_sq][:, 0:T]),
                      reads=[B_wk[i_sq]], writes=[B_wk[i_sq]])
                tr.op(DVE, lambda e: e.tensor_scalar(out=wk[i_y][:, 0:T], in0=wk[i_y][:, 0:T], scalar1=gc[:, g:g + 1],
                                                      scalar2=None, op0=ALU.mult),
                      reads=[B_wk[i_y]], writes=[B_wk[i_y]])
                tr.op(DVE, lambda e: e.tensor_tensor(out=out_ap, in0=wk[i_y][:, 0:T], in1=wk[i_sq][:, 0:T],
                                                      op=ALU.mult),
                      reads=[B_wk[i_y], B_wk[i_sq]], writes=[out_buf])
                post()
            defer(ctail)

        def pre_tile(slot, t, own_idx, pumps):
            halo = xh_d[own_idx] if own_idx is not None else None
            rms_prologue(xs[slot][t * T:(t + 1) * T, :], halo, slot * c.NT + t)
            for kvh in range(NKV):
                si = wload(ws_in[s_in_index(c.ok + kvh * 128)], KC * 128)
                o_t, o_b, o_s = stage_out()

                def post(kvh=kvh, o_t=o_t, o_b=o_b, o_s=o_s):
                    tok = tr.dma(POOL, o_s, kS[slot * NKV + kvh, :, t * T:(t + 1) * T], o_t[:], reads=[o_b], writes=[])
                    note_store(tok, kv_toks)
                qk_head(si, ((0, 1, 2) if kvh % 2 == 0 else (3, 4, 5)), gk[:, 0:1], o_t[:], o_b, post, tail_eng=DVE)
                hook()
            flush()
            VW = c.KVW
            for kt in range(NVT):
                si = wload(ws_v[kt], c.MK * VW)

                def f(e, si=si, kt=kt):
                    ins = None
                    for tb in range(TB):
                        for kc in range(c.MK):
                            ins = e.matmul(ps[tb][:, 0:VW], lhsT=hT[:, kt * c.MK + kc, tb * 128:(tb + 1) * 128],
                                           rhs=wb[si][:, kc * VW:(kc + 1) * VW],
                                           start=(kt == 0 and kc == 0), stop=(kt == NVT - 1 and kc == c.MK - 1))
                    return ins
                tr.op(PE, f, reads=[B_wb[si], B_hT], writes=[B_ps[tb] for tb in range(TB)])
                hook()
            for tb in range(TB):
                o_t, o_b, o_s = stage_out()
                tr.op(ACT, lambda e: e.activation(out=o_t[:, 0:VW], in_=ps[tb][:, 0:VW], func=AF.Copy),
                      reads=[B_ps[tb]], writes=[o_b])
                ktile = t * TB + tb
                dst = vS[slot * NKV:(slot + 1) * NKV, :, ktile * 128:(ktile + 1) * 128].rearrange("h p d -> p h d")
                srcv = o_t[:, 0:VW].rearrange("p (h d) -> p h d", h=NKV)
                tok = tr.dma(POOL, o_s, dst, srcv, reads=[o_b], writes=[])
                note_store(tok, kv_toks)
            if own_idx is None:
                return
            for h in range(NH):
                si = wload(ws_in[s_in_index(c.oq + h * 128)], KC * 128)
                o_t, o_b, o_s = stage_out()

                def postq(h=h, o_t=o_t, o_b=o_b, o_s=o_s):
                    tok = tr.dma(POOL, o_s, qS[own_idx, :, h * T:(h + 1) * T], o_t[:], reads=[o_b], writes=[])
                    note_store(tok, kv_toks)
                qk_head(si, ((0, 1, 2) if h % 2 == 0 else (3, 4, 5)), gq[:, 0:1], o_t[:], o_b, postq, tail_eng=DVE)
                hook()
            for g in range(CG):
                o_t, o_b, o_s = stage_out()

                def postc(g=g, o_t=o_t, o_b=o_b, o_s=o_s):
                    tok = tr.dma(POOL, o_s, cS[own_idx, :, g * T:(g + 1) * T], o_t[:], reads=[o_b], writes=[])
                    note_store(tok, kv_toks)
                conv_group(g, o_t[:], o_b, postc)
                hook()
            flush()

        def main_tile(slot, t, own_idx):
            x_src = xs[slot][t * T:(t + 1) * T, :]
            for tb in range(TB):
                tr.dma(SP, d_x[tb], xres[:, tb, :], x_src[tb * 128:(tb + 1) * 128, :], writes=[B_xr[tb]])
            transfer([B_gfin], [B_qT])
            tr.dma(SP, d_qs, qT[:, :], qS[own_idx], writes=[B_qT])
            transfer(B_act, [B_acT])
            tr.dma(SP, d_cs, acT_flat[:, NH * T:(NH + CG) * T], cS[own_idx], writes=[B_acT])
            transfer([B_hT], B_kv)
            scale = 128.0 ** -0.5
            for h in range(NH):
                kvh = h // c.G
                kb = kvh % 2
                if h % c.G == 0:
                    tr.dma(SP, d_kv[kb], kT_view(kb), kS[slot * NKV + kvh], reads=[B_kS[slot]], writes=[B_kv[kb]])
                    tr.dma(SP, d_kv[kb], v_view(kb), vS[slot * NKV + kvh], reads=[B_vS[slot]], writes=[B_kv[kb]])
                bo = 4 + (h % 2)
                bsum = 6 + (h % 2)
                KT = c.KT
                kTv, vv = kT_view(kb), v_view(kb)

                def s_mm(kt):
                    b = kt % 4
                    tr.op(PE, lambda e: e.matmul(ps[b][:, 0:T], lhsT=kTv[:, kt * 128:(kt + 1) * 128],
                                                 rhs=qT[:, h * T:(h + 1) * T], start=True, stop=True),
                          reads=[B_kv[kb], B_qT], writes=[B_ps[b]])
                s_mm(0)
                if KT > 1:
                    s_mm(1)
                for kt in range(KT):
                    b = kt % 4
                    j = (h * KT + kt) % NPT
                    tr.op(ACT, lambda e: e.activation(out=pTb[j][:], in_=ps[b][:, 0:T], func=AF.Exp, scale=scale,
                                                      bias=negc[:, 0:1]),
                          reads=[B_ps[b]], writes=[B_pT[j]])
                    if kt + 2 < KT:
                        s_mm(kt + 2)
                    if kt == min(10, KT - 1):
                        flush()

                    def pv(e, kt=kt, j=j):
                        e.matmul(ps[bo][:, 0:T], lhsT=vv[:, kt * 128:(kt + 1) * 128], rhs=pTb[j][:],
                                 start=(kt == 0), stop=(kt == KT - 1))
                        return e.matmul(ps[bsum][:, 0:T], lhsT=ones_b[:], rhs=pTb[j][:],
                                        start=(kt == 0), stop=(kt == KT - 1))
                    tr.op(PE, pv, reads=[B_kv[kb], B_pT[j]], writes=[B_ps[bo], B_ps[bsum]])
                i_rs, i_a, i_sq = getwk(), getwk(), getwk()
                tr.op(DVE, lambda e: e.reciprocal(out=wk[i_rs][:, 0:T], in_=ps[bsum][:, 0:T]),
                      reads=[B_ps[bsum]], writes=[B_wk[i_rs]])
                tr.op(DVE, lambda e: e.tensor_tensor(out=wk[i_a][:, 0:T], in0=ps[bo][:, 0:T], in1=wk[i_rs][:, 0:T],
                                                     op=ALU.mult), reads=[B_ps[bo], B_wk[i_rs]], writes=[B_wk[i_a]])
                tr.op(POOL, lambda e: e.tensor_tensor(out=wk[i_sq][:, 0:T], in0=wk[i_a][:, 0:T], in1=wk[i_a][:, 0:T],
                                                      op=ALU.mult), reads=[B_wk[i_a]], writes=[B_wk[i_sq]])
                def atail(h=h, bsum=bsum, i_a=i_a, i_sq=i_sq):
                    tr.op(PE, lambda e: e.matmul(ps[bsum][:, 0:T], lhsT=onesm[:], rhs=wk[i_sq][:, 0:T], start=True, stop=True),
                          reads=[B_wk[i_sq]], writes=[B_ps[bsum]])
                    tr.op(ACT, lambda e: e.activation(out=wk[i_sq][:, 0:T], in_=ps[bsum][:, 0:T], func=AF.Ln,
                                                      bias=eps_t[:, 0:1], scale=1.0), reads=[B_ps[bsum]], writes=[B_wk[i_sq]])
                    tr.op(ACT, lambda e: e.activation(out=wk[i_sq][:, 0:T], in_=wk[i_sq][:, 0:T], func=AF.Exp, scale=-0.5),
                          reads=[B_wk[i_sq]], writes=[B_wk[i_sq]])
                    tr.op(POOL, lambda e: e.tensor_scalar(out=wk[i_a][:, 0:T], in0=wk[i_a][:, 0:T], scalar1=ga[:, h:h + 1],
                                                          scalar2=None, op0=ALU.mult),
                          reads=[B_wk[i_a]], writes=[B_wk[i_a]])
                    tr.op(POOL, lambda e: e.tensor_tensor(out=acT[:, h, :], in0=wk[i_a][:, 0:T], in1=wk[i_sq][:, 0:T],
                                                          op=ALU.mult),
                          reads=[B_wk[i_a], B_wk[i_sq]], writes=[B_acT])
                pend["f"] = atail
            flush()
            transfer([B_qT], [B_gfin])
            tr.dma(SP, d_gf, gfin[:, 0:D], gfin_d, writes=[B_gfin])
            CW = c.CW
            inc_stats = (CW == 512)
            if inc_stats:
                stats_reset()
            for n in range(c.NCG):
                banks = [(n % 2) * 4 + tb for tb in range(TB)]
                for kt in range(NOT):
                    si = wload(ws_out[n * NOT + kt], c.MK * CW)

                    def f(e, si=si, kt=kt, banks=banks):
                        ins = None
                        for tb in range(TB):
                            for kc in range(c.MK):
                                ins = e.matmul(ps[banks[tb]][:, 0:CW], lhsT=acT[:, kt * c.MK + kc, tb * 128:(tb + 1) * 128],
                                               rhs=wb[si][:, kc * CW:(kc + 1) * CW],
                                               start=(kt == 0 and kc == 0), stop=(kt == NOT - 1 and kc == c.MK - 1))
                        return ins
                    tr.op(PE, f, reads=[B_wb[si], B_acT], writes=[B_ps[b] for b in banks])
                for tb in range(TB):
                    xs_ap = xres[:, tb, n * CW:(n + 1) * CW]
                    tr.op(DVE, lambda e: e.tensor_tensor(out=xs_ap, in0=ps[banks[tb]][:, 0:CW], in1=xs_ap, op=ALU.add),
                          reads=[B_ps[banks[tb]], B_xr[tb]], writes=[B_xr[tb]])
                    if inc_stats:
                        sq_piece(tb, n)
            transfer(B_kv, [B_hT])
            rms_prologue(None, None, None, stats_done=inc_stats)
            transfer([B_acT], B_act)

            def up(j):
                a = j % 2
                for cbi in range(HCK):
                    si = wload(ws_up[j * HCK + cbi], KC * 128)
                    b = cbi % 2
                    proj_S(si, b)
                    i_r = getwk()
                    tr.op(ACT, lambda e: e.activation(out=wk[i_r][:, 0:T], in_=ps[b][:, 0:T], func=AF.Relu),
                          reads=[B_ps[b]], writes=[B_wk[i_r]])
                    tr.op(POOL, lambda e: e.tensor_tensor(out=act_view(a)[:, cbi * T:(cbi + 1) * T], in0=wk[i_r][:, 0:T],
                                                          in1=wk[i_r][:, 0:T], op=ALU.mult),
                          reads=[B_wk[i_r]], writes=[B_act[a]])
            dn_rr = {"n": 0}

            def down(j):
                a = j % 2
                for n in range(c.NCG):
                    si = wload(ws_dn[j * c.NCG + n], HCK * CW)
                    for tb in range(TB):
                        b = 2 + dn_rr["n"] % 6
                        dn_rr["n"] += 1

                        def f(e, si=si, tb=tb, b=b):
                            ins = None
                            for kc in range(HCK):
                                ins = e.matmul(ps[b][:, 0:CW], lhsT=act_view(a)[:, kc * T + tb * 128: kc * T + (tb + 1) * 128],
                                               rhs=wb[si][:, kc * CW:(kc + 1) * CW], start=(kc == 0), stop=(kc == HCK - 1))
                            return ins
                        tr.op(PE, f, reads=[B_wb[si], B_act[a]], writes=[B_ps[b]])
                        xs_ap = xres[:, tb, n * CW:(n + 1) * CW]
                        tr.op(DVE, lambda e: e.tensor_tensor(out=xs_ap, in0=ps[b][:, 0:CW], in1=xs_ap, op=ALU.add),
                              reads=[B_ps[b], B_xr[tb]], writes=[B_xr[tb]])
                        if inc_stats and j == c.NCH - 1:
                            sq_piece(tb, n)
            up(0)
            for j in range(c.NCH):
                if j + 1 < c.NCH:
                    up(j + 1)
                if inc_stats and j == c.NCH - 1:
                    stats_reset()
                down(j)
            if not inc_stats:
                stats_reset()
                for tb in range(TB):
                    for pc in range(NPC):
                        sq_piece(tb, pc)
            stats_finish()
            FW = min(1024, D)
            tr.wait_tok(DVE, B_rstd.w)
            for tb in range(TB):
                for pc in range(D // FW):
                    xs_ap = xres[:, tb, pc * FW:(pc + 1) * FW]
                    tr.op(DVE, lambda e: e.scalar_tensor_tensor(out=xs_ap, in0=xs_ap, scalar=rstd[:, tb:tb + 1],
                                                                in1=gfin[:, pc * FW:(pc + 1) * FW],
                                                                op0=ALU.mult, op1=ALU.mult),
                          reads=[B_xr[tb], B_rstd, B_gfin], writes=[B_xr[tb]], same_ok=True)
                r0 = own_idx * T + tb * 128
                tok = tr.dma(SP, d_y[tb], y_d[r0:r0 + 128, :], xres[:, tb, :], reads=[B_xr[tb]], writes=[])
                note_store(tok, y_toks)

        transfer(B_kv, [B_hT])
        nonown = [(sl, t) for sl in range(2) if nown[sl] > 0 for t in range(nown[sl], c.NT)]
        owns = [(sl, t) for sl in range(2) for t in range(nown[sl])]
        n_win = nsteps(c.AW, D, KG) + nsteps(3 * c.CWD, D, KG)
        n_b = n_win
        n_pieces = TB * (-(-KC // 4))
        hooks_b = n_pieces + NKV + NVT
        hk["rate"] = n_b / max(1, len(nonown) * hooks_b) * 1.05
        for sl, t in nonown:
            pre_tile(sl, t, None, None)
        pump(max(0, n_b - pumped["n"]))
        for sem, val in store_toks.items():
            tr.wait_tok(SP, (sem, val))
        n_c = max(0, n_rest - pumped["n"])
        hooks_c = hooks_b + NH + 3 * CG
        hk["rate"] = n_c / max(1, len(owns) * hooks_c) * 1.03
        hk["acc"] = 0.0
        for own, (sl, t) in enumerate(owns):
            pre_tile(sl, t, own, None)
        hk["rate"] = 0.0
        for sem, val in list(kv_toks.items()):
            tr.wait_tok(SP, (sem, val))
        p0_barrier()
        for own, (sl, t) in enumerate(owns):
            main_tile(sl, t, own)
        for sem, val in y_toks.items():
            tr.wait_tok(SP, (sem, val))
    return nc


def rope_tables(pos):
    row = (pos // GRID_W).astype(np.float32)
    col = (pos % GRID_W).astype(np.float32)
    freqs = (np.float32(ROPE_THETA) ** (-np.arange(0, 64, 2, dtype=np.float32) / np.float32(64))).astype(np.float32)
    d = np.arange(128)
    f = freqs[d % 32]
    p = np.where((d // 64)[:, None] == 0, row[None, :], col[None, :]).astype(np.float32)
    ang = (p * f[:, None]).astype(np.float32)
    return np.cos(ang).astype(np.float32), np.sin(ang).astype(np.float32)


def make_consts():
    ident = np.eye(128, dtype=np.float32)
    rm = np.zeros((128, 128), np.float32)
    for m in range(128):
        if (m % 64) < 32:
            rm[m + 32, m] = -1.0
        else:
            rm[m - 32, m] = 1.0
    return ident, rm


def plan_cores(cfg, n_seq, n_cores):
    c = cfg
    plans = []
    for core in range(n_cores):
        if c.nB > 0:
            nAseq = n_cores * c.nA // c.NT
            sa = core * c.nA // c.NT
            ta = [(core * c.nA) % c.NT + i for i in range(c.nA)]
            sbq = nAseq + core * c.nB // c.NT
            tbl = [(core * c.nB) % c.NT + i for i in range(c.nB)]
            plans.append([(sa, ta), (sbq, tbl)])
        else:
            sa = core * c.nA // c.NT
            ta = [(core * c.nA) % c.NT + i for i in range(c.nA)]
            plans.append([(sa, ta), (sa, [])])
    return plans


def host_inputs(cfg, xseqs, w, plans):
    c = cfg
    D, S = c.D, c.S
    ident, rm = make_consts()
    common = {
        "w_in": np.ascontiguousarray(w["w_in"]), "w_out": np.ascontiguousarray(w["w_out"]),
        "w_up": np.ascontiguousarray(w["w_up"]), "w_down": np.ascontiguousarray(w["w_down"]),
        "gmix": np.ascontiguousarray(w["mix_norm"].reshape(c.KC, 128).T),
        "gmlp": np.ascontiguousarray(w["mlp_norm"].reshape(c.KC, 128).T),
        "gq": np.ascontiguousarray(w["q_norm"].reshape(128, 1)),
        "gk": np.ascontiguousarray(w["k_norm"].reshape(128, 1)),
        "convw": np.ascontiguousarray(w["conv_w"].reshape(3, c.CG, 128).transpose(2, 1, 0).reshape(128, c.CG * 3)),
        "ga": np.ascontiguousarray(w["attn_grp_norm"].reshape(c.NH, 128).T),
        "gc": np.ascontiguousarray(w["conv_grp_norm"].reshape(c.CG, 128).T),
        "gfin": np.ascontiguousarray(np.broadcast_to(w["final_norm"].reshape(1, D), (128, D))),
        "ident": ident, "rm": rm,
    }
    in_maps = []
    for plan in plans:
        m = dict(common)
        halos = []
        ct = np.zeros((2 * c.NT, 128, T), np.float32)
        st = np.zeros((2 * c.NT, 128, T), np.float32)
        for slot, (seq, tiles) in enumerate(plan):
            order = list(tiles) + [t for t in range(c.NT) if t not in tiles]
            pos = np.concatenate([np.arange(t * T, (t + 1) * T) for t in order])
            xs = xseqs[seq]
            m["xA" if slot == 0 else "xB"] = np.ascontiguousarray(xs[pos])
            cs, sn = rope_tables(pos)
            ct[slot * c.NT:(slot + 1) * c.NT] = cs.reshape(128, c.NT, T).transpose(1, 0, 2)
            st[slot * c.NT:(slot + 1) * c.NT] = sn.reshape(128, c.NT, T).transpose(1, 0, 2)
            for t in tiles:
                prev = xs[t * T - 1] if t * T - 1 >= 0 else np.zeros(D, np.float32)
                nxt = xs[(t + 1) * T] if (t + 1) * T < S else np.zeros(D, np.float32)
                halos.append(prev)
                halos.append(nxt)
        hh = np.stack(halos).astype(np.float32).reshape(len(halos) // 2, 2, c.KC, 128)
        m["xh"] = np.ascontiguousarray(hh.transpose(0, 3, 1, 2).reshape(len(halos) // 2, 128, 2 * c.KC))
        m["ctab"], m["stab"] = ct, st
        in_maps.append(m)
    return in_maps


def run(cfg, xseqs, w, n_cores=8):
    plans = plan_cores(cfg, xseqs.shape[0], n_cores)
    in_maps = host_inputs(cfg, xseqs, w, plans)
    nc = build(cfg)
    res = run_bass_kernel_spmd(nc, in_maps, core_ids=list(range(n_cores)))
    out = np.zeros_like(xseqs)
    for core, plan in enumerate(plans):
        y = res.results[core]["y"]
        o = 0
        for seq, tiles in plan:
            for t in tiles:
                out[seq, t * T:(t + 1) * T] = y[o * T:(o + 1) * T]
                o += 1
    return out


def kernel(x_prompt, x_sample, w_in, q_norm, k_norm, conv_w, attn_grp_norm, conv_grp_norm, w_out, mix_norm,
           mlp_norm, w_up, w_down, final_norm):
    f = lambda a: np.asarray(a, dtype=np.float32)
    xseqs = np.concatenate([f(x_prompt), f(x_sample)], axis=0)
    w = {"w_in": f(w_in)[0], "w_out": f(w_out)[0], "w_up": f(w_up)[0], "w_down": f(w_down)[0],
         "mix_norm": f(mix_norm)[0], "mlp_norm": f(mlp_norm)[0], "q_norm": f(q_norm)[0], "k_norm": f(k_norm)[0],
         "conv_w": f(conv_w)[0], "attn_grp_norm": f(attn_grp_norm)[0], "conv_grp_norm": f(conv_grp_norm)[0],
         "final_norm": f(final_norm)}
    out = run(FULL, xseqs, w, 8)
    nb = x_prompt.shape[0]
    return (np.ascontiguousarray(out[:nb]), np.ascontiguousarray(out[nb:]))
```
